# Optimizing a Trainium2 kernel written in Bass

```python
import jax, jax.numpy as jnp
from jax import lax
import numpy as np

D_MODEL = 1024
BATCH = 8
SEQ = 4096
DEPTH = 1

CHUNK = 64
Q_BLOCK = 128
LRU_WIDTH = 512
LRU_BLOCKS = 8
LRU_BLOCK_DIM = LRU_WIDTH // LRU_BLOCKS
CONV_WIDTH = 4
LRU_C = 8.0
MLA_HEADS = 8
QK_NOPE_DIM = 64
QK_ROPE_DIM = 32
V_HEAD_DIM = 64
Q_LORA_RANK = 384
KV_LORA_RANK = 256
ROPE_THETA = 10000.0
MLA_WIDTH = MLA_HEADS * V_HEAD_DIM
D_MIX = LRU_WIDTH + MLA_WIDTH
IN_COLS = 2 * LRU_WIDTH + Q_LORA_RANK + KV_LORA_RANK + QK_ROPE_DIM
D_FF = 2816
EPS = 1e-6

kernel_name = "hybrid_rglru_mla_macaron_sandwich"


def rms_norm(x, g):
    xf = x.astype(jnp.float32)
    y = xf * lax.rsqrt(jnp.mean(xf * xf, axis=-1, keepdims=True) + EPS)
    return (y * g.astype(jnp.float32)).astype(x.dtype)


def swiglu(x, w_gate, w_up, w_down):
    return (jax.nn.silu(x @ w_gate) * (x @ w_up)) @ w_down


def causal_depthwise_conv(x, w, b):
    s = x.shape[1]
    xp = jnp.pad(x, ((0, 0), (CONV_WIDTH - 1, 0), (0, 0)))
    y = b
    for k in range(CONV_WIDTH):
        y = y + xp[:, k:k + s, :] * w[k]
    return y


def rg_lru(x, w_a, b_a, w_x, b_x, lam):
    bsz, s, _ = x.shape
    xb = x.reshape(bsz, s, LRU_BLOCKS, LRU_BLOCK_DIM)
    r = jax.nn.sigmoid((jnp.einsum('bsnd,nde->bsne', xb, w_a).reshape(bsz, s, LRU_WIDTH) + b_a).astype(jnp.float32))
    i = jax.nn.sigmoid((jnp.einsum('bsnd,nde->bsne', xb, w_x).reshape(bsz, s, LRU_WIDTH) + b_x).astype(jnp.float32))
    log_a = -LRU_C * r * jax.nn.softplus(-lam.astype(jnp.float32))
    a = jnp.exp(log_a)
    mult = jnp.sqrt(-jnp.expm1(2.0 * log_a))
    u = mult * (i * x.astype(jnp.float32))

    def combine(left, right):
        a1, b1 = left
        a2, b2 = right
        return a1 * a2, a2 * b1 + b2

    _, h = lax.associative_scan(combine, (a, u), axis=1)
    return h.astype(x.dtype)


def rope(x, cos, sin):
    half = x.shape[-1] // 2
    x1, x2 = x[..., :half], x[..., half:]
    return jnp.concatenate([x1 * cos - x2 * sin, x2 * cos + x1 * sin], axis=-1).astype(x.dtype)


def mla(q_lat, kv_lat, k_rope_raw, positions, q_a_norm, w_q_b, kv_a_norm, w_kv_b):
    bsz, s, _ = q_lat.shape
    q = (rms_norm(q_lat, q_a_norm) @ w_q_b).reshape(bsz, s, MLA_HEADS, QK_NOPE_DIM + QK_ROPE_DIM)
    q_nope, q_pe = q[..., :QK_NOPE_DIM], q[..., QK_NOPE_DIM:]
    kv = (rms_norm(kv_lat, kv_a_norm) @ w_kv_b).reshape(bsz, s, MLA_HEADS, QK_NOPE_DIM + V_HEAD_DIM)
    k_nope, v = kv[..., :QK_NOPE_DIM], kv[..., QK_NOPE_DIM:]

    inv_freq = 1.0 / (ROPE_THETA ** (jnp.arange(0, QK_ROPE_DIM, 2, dtype=jnp.float32) / QK_ROPE_DIM))
    ang = positions.astype(jnp.float32)[..., None] * inv_freq
    cos, sin = jnp.cos(ang), jnp.sin(ang)
    q_pe = rope(q_pe, cos[:, :, None, :], sin[:, :, None, :])
    k_pe = rope(k_rope_raw, cos, sin)

    scale = (QK_NOPE_DIM + QK_ROPE_DIM) ** -0.5
    n_blocks = s // Q_BLOCK
    qn_blocks = q_nope.reshape(bsz, n_blocks, Q_BLOCK, MLA_HEADS, QK_NOPE_DIM).transpose(1, 0, 2, 3, 4)
    qp_blocks = q_pe.reshape(bsz, n_blocks, Q_BLOCK, MLA_HEADS, QK_ROPE_DIM).transpose(1, 0, 2, 3, 4)
    key_chunk = jnp.arange(s) // CHUNK

    def attend(args):
        qn, qp, blk = args
        sc = (jnp.einsum('bqhd,bkhd->bhqk', qn, k_nope)
              + jnp.einsum('bqhd,bkd->bhqk', qp, k_pe)).astype(jnp.float32) * scale
        q_chunk = (blk * Q_BLOCK + jnp.arange(Q_BLOCK)) // CHUNK
        mask = key_chunk[None, :] <= q_chunk[:, None]
        sc = jnp.where(mask[None, None], sc, -1e30)
        p = jax.nn.softmax(sc, axis=-1).astype(v.dtype)
        return jnp.einsum('bhqk,bkhd->bqhd', p, v)

    out = lax.map(attend, (qn_blocks, qp_blocks, jnp.arange(n_blocks)))
    return out.transpose(1, 0, 2, 3, 4).reshape(bsz, s, MLA_WIDTH)


def hybrid_mixer(h, positions, w_in, conv_w, conv_b, w_lru_a, b_lru_a, w_lru_x, b_lru_x,
                 lru_lambda, q_a_norm, w_q_b, kv_a_norm, w_kv_b, w_out):
    proj = h @ w_in
    cuts = [LRU_WIDTH, 2 * LRU_WIDTH, 2 * LRU_WIDTH + Q_LORA_RANK,
            2 * LRU_WIDTH + Q_LORA_RANK + KV_LORA_RANK]
    x_lru, gate, q_lat, kv_lat, k_rope_raw = jnp.split(proj, cuts, axis=-1)
    y_lru = rg_lru(causal_depthwise_conv(x_lru, conv_w, conv_b),
                   w_lru_a, b_lru_a, w_lru_x, b_lru_x, lru_lambda) * jax.nn.gelu(gate)
    y_mla = mla(q_lat, kv_lat, k_rope_raw, positions, q_a_norm, w_q_b, kv_a_norm, w_kv_b)
    return jnp.concatenate([y_lru, y_mla], axis=-1) @ w_out


def setup_inputs(seed: int = 0) -> dict:
    key = jax.random.key(seed)
    ks = iter(jax.random.split(key, 40))
    f32 = jnp.float32

    def nrm(shape, fan_in):
        return jax.random.normal(next(ks), shape, f32) * (fan_in ** -0.5)

    def gain(shape):
        return 1.0 + 0.01 * jax.random.normal(next(ks), shape, f32)

    def bias(shape):
        return 0.01 * jax.random.normal(next(ks), shape, f32)

    L = DEPTH
    x = jax.random.normal(next(ks), (BATCH, SEQ, D_MODEL), f32)
    offset = jax.random.randint(next(ks), (BATCH, 1), 0, 65536, dtype=jnp.int32)
    positions = (offset + jnp.arange(SEQ, dtype=jnp.int32)[None, :]).astype(jnp.int32)

    a0 = jax.random.uniform(next(ks), (L, LRU_WIDTH), f32, 0.9, 0.999)
    sa = a0 ** (1.0 / LRU_C)
    lru_lambda = jnp.log(sa) - jnp.log1p(-sa)

    return {
        "x": x,
        "positions": positions,
        "g_ffn1_pre": gain((L, D_MODEL)),
        "g_ffn1_post": gain((L, D_MODEL)),
        "w_ffn1_gate": nrm((L, D_MODEL, D_FF), D_MODEL),
        "w_ffn1_up": nrm((L, D_MODEL, D_FF), D_MODEL),
        "w_ffn1_down": nrm((L, D_FF, D_MODEL), D_FF),
        "g_mix_pre": gain((L, D_MODEL)),
        "g_mix_post": gain((L, D_MODEL)),
        "w_in": nrm((L, D_MODEL, IN_COLS), D_MODEL),
        "conv_w": nrm((L, CONV_WIDTH, LRU_WIDTH), CONV_WIDTH),
        "conv_b": bias((L, LRU_WIDTH)),
        "w_lru_a": nrm((L, LRU_BLOCKS, LRU_BLOCK_DIM, LRU_BLOCK_DIM), LRU_BLOCK_DIM),
        "b_lru_a": bias((L, LRU_WIDTH)),
        "w_lru_x": nrm((L, LRU_BLOCKS, LRU_BLOCK_DIM, LRU_BLOCK_DIM), LRU_BLOCK_DIM),
        "b_lru_x": bias((L, LRU_WIDTH)),
        "lru_lambda": lru_lambda,
        "q_a_norm": gain((L, Q_LORA_RANK)),
        "w_q_b": nrm((L, Q_LORA_RANK, MLA_HEADS * (QK_NOPE_DIM + QK_ROPE_DIM)), Q_LORA_RANK),
        "kv_a_norm": gain((L, KV_LORA_RANK)),
        "w_kv_b": nrm((L, KV_LORA_RANK, MLA_HEADS * (QK_NOPE_DIM + V_HEAD_DIM)), KV_LORA_RANK),
        "w_out": nrm((L, D_MIX, D_MODEL), D_MIX),
        "g_ffn2_pre": gain((L, D_MODEL)),
        "g_ffn2_post": gain((L, D_MODEL)),
        "w_ffn2_gate": nrm((L, D_MODEL, D_FF), D_MODEL),
        "w_ffn2_up": nrm((L, D_MODEL, D_FF), D_MODEL),
        "w_ffn2_down": nrm((L, D_FF, D_MODEL), D_FF),
    }


def reference(x, positions, g_ffn1_pre, g_ffn1_post, w_ffn1_gate, w_ffn1_up, w_ffn1_down,
              g_mix_pre, g_mix_post, w_in, conv_w, conv_b, w_lru_a, b_lru_a, w_lru_x, b_lru_x,
              lru_lambda, q_a_norm, w_q_b, kv_a_norm, w_kv_b, w_out,
              g_ffn2_pre, g_ffn2_post, w_ffn2_gate, w_ffn2_up, w_ffn2_down):
    h = x
    for l in range(DEPTH):
        f = swiglu(rms_norm(h, g_ffn1_pre[l]), w_ffn1_gate[l], w_ffn1_up[l], w_ffn1_down[l])
        h = h + 0.5 * rms_norm(f, g_ffn1_post[l])
        m = hybrid_mixer(rms_norm(h, g_mix_pre[l]), positions, w_in[l], conv_w[l], conv_b[l],
                         w_lru_a[l], b_lru_a[l], w_lru_x[l], b_lru_x[l], lru_lambda[l],
                         q_a_norm[l], w_q_b[l], kv_a_norm[l], w_kv_b[l], w_out[l])
        h = h + rms_norm(m, g_mix_post[l])
        f = swiglu(rms_norm(h, g_ffn2_pre[l]), w_ffn2_gate[l], w_ffn2_up[l], w_ffn2_down[l])
        h = h + 0.5 * rms_norm(f, g_ffn2_post[l])
    return h
```

```python
import os
import math
from contextlib import ExitStack

import numpy as np
import concourse.bass as bass
import concourse.mybir as mybir
from concourse.bass_utils import run_bass_kernel_spmd

F32 = mybir.dt.float32
BF16 = mybir.dt.bfloat16
I32 = mybir.dt.int32
AF = mybir.ActivationFunctionType
ALU = mybir.AluOpType

D = 1024
S = 4096
T = 512
NT = S // T
DFF = 2816
HALVES = [(0, 12), (12, 10)]
EPS = 1e-6
NV = 88
SCALE = 96.0 ** -0.5
MAGIC = 12582912.0
TWO_PI = 2.0 * math.pi


def _split_2pi():
    c1 = 6.28125
    r = TWO_PI - c1
    e = math.floor(math.log2(abs(r)))
    q = 2.0 ** (e - 9)
    c2 = round(r / q) * q
    c3 = float(np.float32(TWO_PI - c1 - c2))
    return c1, float(np.float32(c2)), c3


LRU_SPREAD = os.environ.get("MK_LRU_SPREAD", "1") == "1"
PREFETCH = os.environ.get("MK_PREFETCH", "1") == "1"
STRICT_SYNC = os.environ.get("MK_STRICT_SYNC", "1") == "1"
C1, C2, C3 = _split_2pi()
PI_CL = float(np.float32(3.1415925))

WSPEC = {
    "gu1": (11, 4096), "wd1a": (4, 3072), "wd1b": (4, 2560),
    "win": (4, 4096), "wqb": (1, 3072), "wkvb": (1, 2048), "wout": (2, 4096),
    "gu2": (11, 4096), "wd2a": (4, 3072), "wd2b": (4, 2560),
}


def tile_block_seq():
    seq = []
    for f in (1, 2):
        fs = []
        for p in range(6):
            fs.append(("gu%d" % f, p))
        for q in range(4):
            fs.append(("wd%da" % f, q))
        for p in range(6, 11):
            fs.append(("gu%d" % f, p))
        for q in range(4):
            fs.append(("wd%db" % f, q))
        if f == 1:
            seq += fs
            seq.append(("win", 0))
            seq.append(("win", 1))
            seq.append(("wqb", 0))
            seq.append(("wkvb", 0))
            seq.append(("win", 2))
            seq.append(("win", 3))
            for b in range(2):
                seq.append(("wout", b))
        else:
            seq += fs
    return seq


class Prog:
    def __init__(self):
        self.ops = []
        self.lastw = {}
        self.readers = {}
        self.dma_cnt = {}
        self.epos = {}

    def add(self, eng, fn, reads=(), writes=(), dma=None):
        idx = len(self.ops)
        deps = set()
        raw = set()
        for r in reads:
            w = self.lastw.get(r)
            if w is not None:
                deps.add(w)
                raw.add(w)
        for r in writes:
            w = self.lastw.get(r)
            if w is not None:
                deps.add(w)
            rl = self.readers.get(r)
            if rl:
                deps.update(rl)
        deps.discard(idx)
        for r in reads:
            self.readers.setdefault(r, []).append(idx)
        for r in writes:
            self.lastw[r] = idx
            self.readers[r] = []
        pos = self.epos.get(eng, 0)
        self.epos[eng] = pos + 1
        raw_same = set()
        if eng in ("act", "dve", "pool") and dma is None:
            for m in (deps if STRICT_SYNC else raw):
                om = self.ops[m]
                if om["eng"] == eng and om["dma"] is None and (STRICT_SYNC or pos - om["pos"] <= 2):
                    raw_same.add(m)
        op = {"eng": eng, "fn": fn, "deps": deps, "dma": dma, "sig": False, "pos": pos, "raw_same": raw_same}
        if dma is not None:
            self.dma_cnt[dma] = self.dma_cnt.get(dma, 0) + 16
            op["dcnt"] = self.dma_cnt[dma]
        self.ops.append(op)
        return idx

    def finalize(self):
        ops = self.ops
        for op in ops:
            for m in op["deps"]:
                om = ops[m]
                if om["dma"] is None and (om["eng"] != op["eng"] or m in op["raw_same"]):
                    om["sig"] = True
        cnt = {}
        for op in ops:
            if op["dma"] is None and op["sig"]:
                e = op["eng"]
                cnt[e] = cnt.get(e, 0) + 1
                op["scnt"] = cnt[e]

    def emit(self, eng_name, e, eng_sems, dma_sems):
        ops = self.ops
        known = {}
        for op in ops:
            if op["eng"] != eng_name:
                continue
            waits = {}
            for m in op["deps"]:
                om = ops[m]
                if om["dma"] is not None:
                    key = ("d", om["dma"])
                    val = om["dcnt"]
                elif om["eng"] != eng_name or m in op["raw_same"]:
                    key = ("e", om["eng"])
                    val = om["scnt"]
                else:
                    continue
                if val > waits.get(key, 0):
                    waits[key] = val
            for key, val in waits.items():
                if known.get(key, 0) < val:
                    sem = dma_sems[key[1]] if key[0] == "d" else eng_sems[key[1]]
                    e.wait_ge(sem, val)
                    known[key] = val
            if op["fn"] is not None:
                ins = op["fn"](e)
                if op["dma"] is not None:
                    ins.then_inc(dma_sems[op["dma"]], 16)
                elif op["sig"]:
                    ins.then_inc(eng_sems[eng_name], 1)


def build(debug_stop=None, ntiles=NT):
    nc = bass.Bass("TRN2", target_bir_lowering=False)
    P = Prog()

    xT = nc.dram_tensor("xT", [D, S], F32, kind="ExternalInput").ap()
    pos = nc.dram_tensor("pos", [1, S], I32, kind="ExternalInput").ap()
    vecs = nc.dram_tensor("vecs", [128, NV], F32, kind="ExternalInput").ap()
    bdw = nc.dram_tensor("bdw", [128, 1024], F32, kind="ExternalInput").ap()
    outT = nc.dram_tensor("outT", [D, S], F32, kind="ExternalOutput").ap()
    wf = {}
    wb = {}
    for name, (nb, fr) in WSPEC.items():
        wf[name] = nc.dram_tensor(name, [nb, 128, fr], F32, kind="ExternalInput").ap()
        wb[name] = nc.dram_tensor(name + "_bf", [nb, 128, fr], BF16, kind="Internal").ap()
    xT3 = xT.rearrange("(c p) t -> p c t", p=128)
    outT3 = outT.rearrange("(c p) t -> p c t", p=128)

    tseq = tile_block_seq()
    blocks = []
    for i in range(ntiles):
        blocks += tseq

    dma_keys = [("w", s) for s in range(3)] + [("cv", k) for k in range(8)] + [("x", c) for c in range(8)] + [("xs", 0), ("xs", 1), ("stg", 0), ("stg", 1), ("wbk", 0), ("wbk", 1), ("wbk", 2), "pos", "st", "c0", "c1", "dbg"]

    with ExitStack() as es:
        def sb(name, shape, dt):
            return es.enter_context(nc.sbuf_tensor(name, shape, dt))

        KT = sb("KT", [128, 8 * S], BF16)
        VC = sb("VC", [128, 32 * 768], BF16)
        H = sb("H", [128, 8 * T], F32)
        XN = sb("XN", [128, 8 * T], BF16)
        HIDa = sb("HID", [128, 12 * T], BF16)
        Fa = sb("F", [128, 8 * T], F32)
        W = sb("W", [128, 3 * 4096], BF16)
        RS = sb("RS", [128, T], F32)
        SQa = sb("SQ", [128, 2 * T], BF16)
        SGa = sb("SG", [128, 2 * T], F32)
        TTa = sb("TT", [128, 2 * T], F32)
        VEC = sb("VEC", [128, NV], F32)
        ONES = sb("ONES", [128, 128], BF16)
        ZEROS = sb("ZEROS", [128, 128], BF16)
        BD = sb("BD", [128, 1024], BF16)
        MA = sb("MA", [1, 128], BF16)
        MB = sb("MB", [1, 64], BF16)
        HALO = sb("HALO", [128, 12], F32)
        HS = sb("HS", [128, 4], F32)
        CL = sb("CL", [128, 16], F32)
        GP = sb("GP", [128, 24], F32)
        LT = sb("LT", [128, 24], F32)
        YL = sb("YL", [128, 4 * T], BF16)
        PSB = [es.enter_context(nc.psum_tensor("ps%d" % b, [128, 512], F32))[:] for b in range(8)]

        eng_sems = {k: es.enter_context(nc.semaphore("s_" + k)) for k in ("pe", "act", "dve", "pool", "sp")}
        dma_sems = {}
        for k in dma_keys:
            nm = "d_" + (k if isinstance(k, str) else "%s%d" % k)
            dma_sems[k] = es.enter_context(nc.semaphore(nm))

        KT3 = KT[:].rearrange("p (h t) -> p h t", h=8)
        VC3 = VC[:].rearrange("p (c x) -> p c x", x=768)
        H3 = H[:].rearrange("p (c t) -> p c t", c=8)
        XN3 = XN[:].rearrange("p (c t) -> p c t", c=8)
        HID3 = HIDa[:].rearrange("p (c t) -> p c t", c=12)
        F3 = Fa[:].rearrange("p (c t) -> p c t", c=8)
        W3 = W[:].rearrange("p (s x) -> p s x", s=3)
        YL3 = YL[:].rearrange("p (c t) -> p c t", c=4)
        SQ = [SQa[:, 0:T], SQa[:, T:2 * T]]
        SG = [SGa[:, 0:T], SGa[:, T:2 * T]]
        QT3 = Fa[:, 0:2048].bitcast(BF16).rearrange("p (h t) -> p h t", h=8)
        PT3 = Fa[:, 2048:2816].bitcast(BF16).rearrange("p (b t) -> p b t", b=3)
        PT6 = Fa[:, 2048:3584].bitcast(BF16).rearrange("p (b t) -> p b t", b=6)
        QN3 = Fa[:, 2816:3584].bitcast(BF16).rearrange("p (c t) -> p c t", c=3)
        KVN3 = Fa[:, 3584:4096].bitcast(BF16).rearrange("p (c t) -> p c t", c=2)
        XL = HIDa[:, 0:1536].bitcast(F32)
        CV = HIDa[:, 1536:2560].bitcast(F32)
        CVB = HIDa[:, 2560:3072]
        RR_ = HIDa[:, 3072:4096].bitcast(F32)
        II_ = HIDa[:, 4096:5120].bitcast(F32)
        AA_ = HIDa[:, 5120:6144].bitcast(F32)
        PIv = HIDa[:, 3072:4096].bitcast(I32)
        GL = SGa[:, 0:T]
        TBL = TTa[:, 0:T]
        RSKV = SQa[:].bitcast(F32)
        YM3 = TTa[:].bitcast(BF16).rearrange("p (c t) -> p c t", c=4)
        T1 = TTa[:, 0:T]
        T2 = TTa[:, T:2 * T]

        def rF(c):
            return [("F", 2 * c), ("F", 2 * c + 1)]
        rF_all = [("F", g) for g in range(16)]
        rQT = lambda h: [("F", h)]
        rPT = lambda b: [("F", 8 + b)]
        rQN = lambda c: [("F", 11 + c)]
        rKVN = lambda c: [("F", 14 + c)]
        rHID = lambda j: [("HID", j)]
        rXLh = []
        rXL = [("HID", 0), ("HID", 1), ("HID", 2)]
        rCV = [("HID", 3), ("HID", 4)]
        rCVB = [("HID", 5)]
        rR = [("HID", 6), ("HID", 7)]
        rI = [("HID", 8), ("HID", 9)]
        rA = [("HID", 10), ("HID", 11)]
        rSG = lambda b: [("SG", 2 * b), ("SG", 2 * b + 1)]
        rGL = rSG(0)
        rTBL = [("TT", 0)]
        rSQ = lambda b: [("SQ", b)]
        rRSKV = [("SQ", 0), ("SQ", 1)]
        rRS = [("RS",)]
        rT1 = [("TT", 0)]
        rT2 = [("TT", 1)]
        rPS = lambda b: [("PS", b)]

        def rW(slot):
            return [("W", slot), ("Wp", slot, 0), ("Wp", slot, 1), ("Wp", slot, 2)]

        def MM(out, lhsT, rhs, start, stop, reads, writes):
            rr = []
            for r in reads:
                if len(r) == 2 and r[0] == "W":
                    rr += rW(r[1])
                else:
                    rr.append(r)
            P.add("pe", lambda e: e.matmul(out, lhsT=lhsT, rhs=rhs, start=start, stop=stop), rr, writes)

        def ACT(out, in_, func, reads, writes, scale=None, bias=None):
            kw = {}
            if scale is not None:
                kw["scale"] = scale
            if bias is not None:
                kw["bias"] = bias
            P.add("act", lambda e: e.activation(out=out, in_=in_, func=func, **kw), reads, writes)

        def TTo(out, in0, in1, op, reads, writes, eng="dve"):
            P.add(eng, lambda e: e.tensor_tensor(out=out, in0=in0, in1=in1, op=op), reads, writes)

        def TS(out, in0, s1, s2, op0, op1, reads, writes, eng="dve"):
            if s2 is None:
                P.add(eng, lambda e: e.tensor_scalar(out=out, in0=in0, scalar1=s1, scalar2=None, op0=op0), reads, writes)
            else:
                P.add(eng, lambda e: e.tensor_scalar(out=out, in0=in0, scalar1=s1, scalar2=s2, op0=op0, op1=op1), reads, writes)

        def STT(out, in0, scalar, in1, op0, op1, reads, writes):
            P.add("dve", lambda e: e.scalar_tensor_tensor(out=out, in0=in0, scalar=scalar, in1=in1, op0=op0, op1=op1), reads, writes)

        def CP(out, in_, reads, writes, eng="dve"):
            P.add(eng, lambda e: e.tensor_copy(out=out, in_=in_), reads, writes)

        def MS(ap, val, writes, eng="dve"):
            P.add(eng, lambda e: e.memset(ap, val), (), writes)

        gstate = {"ga": 0, "next_load": 0, "next_use": 0}

        def ga():
            b = gstate["ga"]
            gstate["ga"] = (b + 1) % 6
            return b

        STG = [VC[:, 3072:11264].bitcast(F32), VC[:, 11264:19456].bitcast(F32)]
        pend_wb = []

        def ensure_loaded(upto):
            while gstate["next_load"] <= min(upto, len(blocks) - 1):
                n = gstate["next_load"]
                name, bi = blocks[n]
                slot = n % 3
                fr = WSPEC[name][1]
                if n < len(tseq):
                    sb_ = n % 2
                    P.add("sp", lambda e, sb_=sb_, fr=fr, name=name, bi=bi: e.dma_start(out=STG[sb_][:, 0:fr], in_=wf[name][bi, :, :]),
                          reads=(), writes=[("STG", sb_)], dma=("stg", sb_))
                    while pend_wb:
                        pend_wb.pop(0)()
                    q1 = (fr * 6 // 64) // 64 * 64
                    q2 = q1 + ((fr - q1) * 17 // 32) // 64 * 64
                    P.add("pool", lambda e, sb_=sb_, slot=slot, q1=q1: e.tensor_copy(out=W3[:, slot, 0:q1], in_=STG[sb_][:, 0:q1]),
                          reads=[("STG", sb_)], writes=[("Wp", slot, 0)])
                    P.add("dve", lambda e, sb_=sb_, slot=slot, q1=q1, q2=q2: e.tensor_copy(out=W3[:, slot, q1:q2], in_=STG[sb_][:, q1:q2]),
                          reads=[("STG", sb_)], writes=[("Wp", slot, 1)])
                    P.add("act", lambda e, sb_=sb_, slot=slot, q2=q2, fr=fr: e.activation(out=W3[:, slot, q2:fr], in_=STG[sb_][:, q2:fr], func=AF.Copy),
                          reads=[("STG", sb_)], writes=[("Wp", slot, 2)])
                    pend_wb.append(lambda slot=slot, fr=fr, name=name, bi=bi: P.add(
                        "sp", lambda e: e.dma_start(out=wb[name][bi, :, :], in_=W3[:, slot, 0:fr]),
                        reads=rW(slot), writes=[("WB", name, bi)], dma=("wbk", slot)))
                else:
                    while pend_wb:
                        pend_wb.pop(0)()
                    P.add("sp", lambda e, slot=slot, fr=fr, name=name, bi=bi: e.dma_start(out=W3[:, slot, 0:fr], in_=wb[name][bi, :, :]),
                          reads=[("WB", name, bi)], writes=rW(slot), dma=("w", slot))
                gstate["next_load"] += 1

        def use_block(expect):
            n = gstate["next_use"]
            gstate["next_use"] += 1
            assert blocks[n][0] == expect, (blocks[n], expect)
            ensure_loaded(n + 2)
            return n % 3

        P.add("sp", lambda e: e.dma_start(out=VEC[:], in_=vecs[:, :]), (), [("VEC",)], dma="c0")
        P.add("pool", lambda e: e.dma_start(out=BD[:], in_=bdw[:, :]), (), [("BD",)], dma="c1")
        MS(ONES[:], 1.0, [("ONES",)])
        MS(ZEROS[:], 0.0, [("ZEROS",)])
        MS(VC[:, 0:3072].rearrange("p (n x) -> p n x", x=192)[:, :, 64:128], 1.0, [("VCones",)])
        MS(MA[0:1, 0:64], 0.0, [("MA",)])
        MS(MA[0:1, 64:128], 1.0, [("MA",)])
        MS(MB[0:1, :], -30000.0, [("MB",)])
        MS(HALO[:], 0.0, [("HALO", c) for c in range(4)])
        MS(HS[:], 0.0, [("HS", c) for c in range(4)])
        TS(GP[:, 0:8], VEC[:, 8:16], 0.5, None, ALU.mult, None, [("VEC",)], [("GP",)])
        TS(GP[:, 8:16], VEC[:, 24:32], 1.0, None, ALU.mult, None, [("VEC",)], [("GP",)])
        TS(GP[:, 16:24], VEC[:, 40:48], 0.5, None, ALU.mult, None, [("VEC",)], [("GP",)])
        rLT = [("LT",)]
        Y_ = LT[:, 0:4]; E_ = LT[:, 4:8]; Z_ = LT[:, 8:12]; Z2_ = LT[:, 12:16]; P_ = LT[:, 16:20]; Q_ = LT[:, 20:24]
        TS(Y_, VEC[:, 81:85], -1.0, None, ALU.mult, None, [("VEC",)], rLT)
        TTo(Q_, Y_, VEC[:, 81:85], ALU.max, rLT + [("VEC",)], rLT)
        ACT(E_, Q_, AF.Exp, rLT, rLT, scale=-1.0)
        TS(Q_, E_, 2.0, None, ALU.add, None, rLT, rLT)
        P.add("dve", lambda e: e.reciprocal(out=Q_, in_=Q_), rLT, rLT)
        TTo(Z_, E_, Q_, ALU.mult, rLT, rLT)
        TTo(Z2_, Z_, Z_, ALU.mult, rLT, rLT)
        MS(P_, 1.0 / 13.0, rLT)
        for kk in (11, 9, 7, 5, 3, 1):
            TTo(P_, P_, Z2_, ALU.mult, rLT, rLT)
            TS(P_, P_, 1.0 / kk, None, ALU.add, None, rLT, rLT)
        TTo(P_, P_, Z_, ALU.mult, rLT, rLT)
        TS(Q_, Y_, 0.0, None, ALU.max, None, rLT, rLT)
        STT(P_, P_, 2.0, Q_, ALU.mult, ALU.add, rLT, rLT)
        TS(CL[:, 0:4], P_, -8.0, None, ALU.mult, None, rLT, [("CL",)])
        TS(CL[:, 4:8], P_, -4.0, None, ALU.mult, None, rLT, [("CL",)])
        TS(CL[:, 8:12], VEC[:, 73:77], 0.5, None, ALU.mult, None, [("VEC",)], [("CL",)])
        TS(CL[:, 12:16], VEC[:, 77:81], 0.5, None, ALU.mult, None, [("VEC",)], [("CL",)])

        def norm_rstd(ssb, n, dst, rdst):
            ACT(dst, PSB[ssb], AF.Ln, rPS(ssb), rdst, scale=1.0 / n, bias=EPS)
            ACT(dst, dst, AF.Exp, rdst, rdst, scale=-0.5)

        def prenorm(gcol):
            ss = ga()
            for c in range(8):
                q = c % 2
                ACT(SQ[q], H3[:, c, :], AF.Square, [("H", c)], rSQ(q))
                MM(PSB[ss], ONES[:], SQ[q], c == 0, c == 7, rSQ(q) + [("ONES",)], rPS(ss))
            norm_rstd(ss, 1024, RS[:], rRS)
            for c in range(8):
                STT(XN3[:, c, :], H3[:, c, :], VEC[:, gcol + c:gcol + c + 1], RS[:], ALU.mult, ALU.mult,
                    [("H", c), ("VEC",)] + rRS, [("XN", c)])

        def pf_pass1(ti_next, c):
            tn = ti_next * T
            stg, rstg, key = (T1, rT1, ("xs", 0)) if c % 2 == 0 else (T2, rT2, ("xs", 1))
            P.add("sp", lambda e: e.dma_start(out=stg, in_=xT3[:, c, tn:tn + T]), reads=(), writes=rstg, dma=key)
            sq = c % 2
            ACT(SQ[sq], stg, AF.Square, rstg, rSQ(sq))
            MM(PSB[7], ONES[:], SQ[sq], c == 0, c == 7, rSQ(sq) + [("ONES",)], rPS(7))

        def pf_rstd():
            norm_rstd(7, 1024, T1, rT1)

        def pf_pass2(ti_next, c):
            tn = ti_next * T
            stg, rstg, key = (T2, rT2, ("xs", 1)) if c % 2 == 0 else (SG[0], rSG(0), ("xs", 0))
            P.add("sp", lambda e: e.dma_start(out=stg, in_=xT3[:, c, tn:tn + T]), reads=(), writes=rstg, dma=key)
            STT(XN3[:, c, :], stg, VEC[:, c:c + 1], T1, ALU.mult, ALU.mult, rstg + [("VEC",)] + rT1, [("XN", c)])

        def ffn(f, gpre_col, gp_col, final, ti, skip_prenorm=False, prefetch_next=None):
            t0 = ti * T
            fpend = []
            if not skip_prenorm:
                prenorm(gpre_col)
            for hf, (j0, nj) in enumerate(HALVES):
                for p in range(nj // 2):
                    slot = use_block("gu%d" % f)
                    Wv = W3[:, slot, :]
                    for jj in range(2):
                        jl = 2 * p + jj
                        bg = ga()
                        for kc in range(8):
                            o = ((jj * 2 + 0) * 8 + kc) * 128
                            MM(PSB[bg], Wv[:, o:o + 128], XN3[:, kc, :], kc == 0, kc == 7, [("W", slot), ("XN", kc)], rPS(bg))
                        bu = ga()
                        for kc in range(8):
                            o = ((jj * 2 + 1) * 8 + kc) * 128
                            MM(PSB[bu], Wv[:, o:o + 128], XN3[:, kc, :], kc == 0, kc == 7, [("W", slot), ("XN", kc)], rPS(bu))
                        sg = jl % 2
                        ACT(SG[sg], PSB[bg], AF.Silu, rPS(bg), rSG(sg))
                        TTo(HID3[:, jl, :], SG[sg], PSB[bu], ALU.mult, rSG(sg) + rPS(bu), rHID(jl))
                    if f == 1 and hf == 0 and p == 1 and debug_stop != "ffn1":
                        rope_tables(ti)
                    if hf == 1 and prefetch_next is not None:
                        for c_ in ((0, 1), (2, 3), (4, 5), (6,), (7,))[p]:
                            pf_pass1(prefetch_next, c_)
                if hf == 1 and prefetch_next is not None:
                    pf_rstd()
                for q in range(4):
                    slot = use_block("wd%d%s" % (f, "ab"[hf]))
                    Wv = W3[:, slot, :]
                    for cc in range(2):
                        c = 2 * q + cc
                        bk = ga()
                        if len(fpend) > 1:
                            fpend.pop(0)()
                        for jl in range(nj):
                            o = (cc * nj + jl) * 128
                            MM(PSB[bk], Wv[:, o:o + 128], HID3[:, jl, :], jl == 0, jl == nj - 1, [("W", slot)] + rHID(jl), rPS(bk))
                        if hf == 0:
                            ACT(F3[:, c, :], PSB[bk], AF.Copy, rPS(bk), rF(c))
                        else:
                            TTo(F3[:, c, :], F3[:, c, :], PSB[bk], ALU.add, rF(c) + rPS(bk), rF(c))
                            sq = c % 2
                            ACT(SQ[sq], F3[:, c, :], AF.Square, rF(c), rSQ(sq))
                            fpend.append(lambda c=c, sq=sq: MM(PSB[6], ONES[:], SQ[sq], c == 0, c == 7, rSQ(sq) + [("ONES",)], rPS(6)))
                            if prefetch_next is not None:
                                pf_pass2(prefetch_next, c)
            while fpend:
                fpend.pop(0)()
            norm_rstd(6, 1024, RS[:], rRS)
            for c in range(8):
                STT(F3[:, c, :], F3[:, c, :], GP[:, gp_col + c:gp_col + c + 1], RS[:], ALU.mult, ALU.mult,
                    rF(c) + rRS + [("GP",)], rF(c))
                aeng = "dve"
                if not final:
                    TTo(H3[:, c, :], H3[:, c, :], F3[:, c, :], ALU.add, [("H", c)] + rF(c), [("H", c)], eng=aeng)
                else:
                    TTo(F3[:, c, :], H3[:, c, :], F3[:, c, :], ALU.add, [("H", c)] + rF(c), rF(c), eng=aeng)

        def store(src3, rsrc, ti):
            t0 = ti * T
            P.add("sp", lambda e: e.dma_start(out=outT3[:, :, t0:t0 + T], in_=src3), reads=rsrc, writes=[("OUT", ti)], dma="st")

        dumps = []

        def dump(ap, reads, c, ti, psl=slice(0, 128)):
            if debug_stop != "mixdump":
                return
            t0_ = ti * T
            dumps.append(("OUTD", ti, c))
            P.add("pool", lambda e: e.dma_start(out=outT3[psl, c, t0_:t0_ + T], in_=ap), reads=reads, writes=[("OUTD", ti, c)], dma="dbg")

        def rope_tables(ti):
            t0 = ti * T
            T1i = TTa[64:128, 0:T].bitcast(I32)
            X = T1[64:128, :]
            K = T2[64:128, :]
            P.add("sp", lambda e: e.dma_start(out=T1i, in_=pos[0:1, t0:t0 + T].partition_broadcast(64)),
                  reads=(), writes=rT1, dma="pos")
            CP(X, T1i, rT1, rT1)
            TS(X, X, VEC[64:128, 85:86], None, ALU.mult, None, rT1 + [("VEC",)], rT1)
            TS(K, X, 1.0 / TWO_PI, MAGIC, ALU.mult, ALU.add, rT1, rT2)
            TS(K, K, MAGIC, None, ALU.subtract, None, rT2, rT2)
            STT(X, K, -C1, X, ALU.mult, ALU.add, rT1 + rT2, rT1)
            STT(X, K, -C2, X, ALU.mult, ALU.add, rT1 + rT2, rT1)
            STT(X, K, -C3, X, ALU.mult, ALU.add, rT1 + rT2, rT1)
            TS(X, X, PI_CL, -PI_CL, ALU.min, ALU.max, rT1, rT1)
            ACT(T2[64:96, :], T1[64:96, :], AF.Sin, rT1, rT2, scale=0.5)
            ACT(T1[96:128, :], T1[96:128, :], AF.Sin, rT1 + [("VEC",)], rT1, scale=VEC[96:128, 86:87])
            TTo(T2[64:96, :], T2[64:96, :], T2[64:96, :], ALU.mult, rT2, rT2)
            TS(T1[64:96, :], T2[64:96, :], -2.0, 1.0, ALU.mult, ALU.add, rT2, rT1)

        def mixer(ti):
            t0 = ti * T
            prenorm(16)

            pending = []

            def flush_pending():
                while pending:
                    pending.pop(0)()

            def win_mm(oc):
                bk = ga()
                for kc in range(8):
                    o = ((oc % 4) * 8 + kc) * 128
                    MM(PSB[bk], win_state["Wv"][:, o:o + 128], XN3[:, kc, :], kc == 0, kc == 7,
                       [("W", win_state["slot"]), ("XN", kc)], rPS(bk))
                return bk

            win_state = {"slot": None, "Wv": None}

            def win_block():
                win_state["slot"] = use_block("win")
                win_state["Wv"] = W3[:, win_state["slot"], :]

            win_block()
            for ci in range(3):
                bk = win_mm(ci)
                flush_pending()
                ACT(QN3[:, ci, :], PSB[bk], AF.Copy, rPS(bk) + [("VEC",)], rQN(ci), scale=VEC[:, 48 + ci:49 + ci])
                sq = ci % 2
                ACT(SQ[sq], PSB[bk], AF.Square, rPS(bk), rSQ(sq))
                pending.append(lambda ci=ci, sq=sq: MM(PSB[6], ONES[:], SQ[sq], ci == 0, ci == 2, rSQ(sq) + [("ONES",)], rPS(6)))
            bk = win_mm(3)
            flush_pending()
            norm_rstd(6, 384, RS[:], rRS)
            for c in range(3):
                TTo(QN3[:, c, :], QN3[:, c, :], RS[:], ALU.mult, rQN(c) + rRS, rQN(c))
            for ci in range(2):
                if ci == 1:
                    win_block()
                    bk = win_mm(4)
                    flush_pending()
                ACT(KVN3[:, ci, :], PSB[bk], AF.Copy, rPS(bk) + [("VEC",)], rKVN(ci), scale=VEC[:, 51 + ci:52 + ci])
                sq = ci % 2
                ACT(SQ[sq], PSB[bk], AF.Square, rPS(bk), rSQ(sq))
                pending.append(lambda ci=ci, sq=sq: MM(PSB[7], ONES[:], SQ[sq], ci == 0, ci == 1, rSQ(sq) + [("ONES",)], rPS(7)))
            bk = win_mm(5)
            flush_pending()
            norm_rstd(7, 256, RSKV, rRSKV)
            for c in range(2):
                TTo(KVN3[:, c, :], KVN3[:, c, :], RSKV, ALU.mult, rKVN(c) + rRSKV, rKVN(c))
            TTo(CV[64:96, :], PSB[bk][64:96, :], TBL[64:96, :], ALU.mult, rPS(bk) + rTBL, rCV)
            TTo(RR_[64:96, :], PSB[bk][96:128, :], TBL[96:128, :], ALU.mult, rPS(bk) + rTBL, rR)
            reng = "dve"
            for h in range(8):
                TTo(KT3[64:96, h, t0:t0 + T], CV[64:96, :], RR_[64:96, :], ALU.add, rCV + rR, [("KTr", ti)], eng=reng)

            slot = use_block("wqb")
            Wv = W3[:, slot, :]
            for h in range(8):
                bk = ga()
                for kc in range(3):
                    o = (h * 3 + kc) * 128
                    MM(PSB[bk], Wv[:, o:o + 128], QN3[:, kc, :], kc == 0, kc == 2, [("W", slot)] + rQN(kc), rPS(bk))
                ACT(QT3[0:64, h, :], PSB[bk][0:64, :], AF.Copy, rPS(bk), rQT(h))
                if h % 2 == 0:
                    ta, tb, rta, rtb = CV, RR_, rCV, rR
                else:
                    ta, tb, rta, rtb = II_, AA_, rI, rA
                TTo(ta[64:96, :], PSB[bk][64:96, :], TBL[64:96, :], ALU.mult, rPS(bk) + rTBL, rta)
                TTo(tb[64:96, :], PSB[bk][96:128, :], TBL[96:128, :], ALU.mult, rPS(bk) + rTBL, rtb)
                TTo(QT3[64:96, h, :], ta[64:96, :], tb[64:96, :], ALU.add, rta + rtb, rQT(h), eng=reng)
            slot = use_block("wkvb")
            Wv = W3[:, slot, :]
            for j in range(4):
                bk = ga()
                for kc in range(2):
                    o = (kc * 4 + j) * 128
                    MM(PSB[bk], Wv[:, o:o + 128], KVN3[:, kc, :], kc == 0, kc == 1, [("W", slot)] + rKVN(kc), rPS(bk))
                ACT(KT3[0:64, 2 * j, t0:t0 + T], PSB[bk][0:64, :], AF.Copy, rPS(bk), [("KTn", ti)])
                ACT(KT3[0:64, 2 * j + 1, t0:t0 + T], PSB[bk][64:128, :], AF.Copy, rPS(bk), [("KTn", ti)])
            for s_ in range(4):
                bk = ga()
                for kc in range(2):
                    MM(PSB[bk], KVN3[:, kc, s_ * 128:(s_ + 1) * 128], Wv[:, 1024 + kc * 512:1024 + (kc + 1) * 512],
                       kc == 0, kc == 1, [("W", slot)] + rKVN(kc), rPS(bk))
                kcg = 4 * ti + s_
                vdst = VC3[:, kcg, :].rearrange("p (j x) -> p j x", x=192)
                vsrc = PSB[bk].rearrange("p (j e d) -> p j e d", j=4, e=2)
                CP(vdst[:, :, 0:64], vsrc[:, :, 0, :], rPS(bk), [("V", kcg)])
                ACT(vdst[:, :, 128:192], vsrc[:, :, 1, :], AF.Copy, rPS(bk), [("V", kcg)])


            def lru_gen():
                for c in range(4):
                    ocx = 8 + 2 * c
                    if ocx % 4 == 0:
                        win_block()
                    bkx = win_mm(ocx)
                    bkg = win_mm(ocx + 1)
                    CP(XL[:, 0:3], HALO[:, 3 * c:3 * c + 3], [("HALO", c)], rXL)
                    ACT(XL[:, 3:3 + T], PSB[bkx], AF.Copy, rPS(bkx), rXL)
                    CP(HALO[:, 3 * c:3 * c + 3], XL[:, T:T + 3], rXL, [("HALO", c)])
                    ACT(GL, PSB[bkg], AF.Gelu_apprx_tanh, rPS(bkg), rGL)
                    yield
                    TS(CV, XL[:, 0:T], VEC[:, 53 + c:54 + c], VEC[:, 69 + c:70 + c], ALU.mult, ALU.add,
                       rXL + [("VEC",)], rCV)
                    for k in range(1, 4):
                        STT(CV, XL[:, k:k + T], VEC[:, 53 + 4 * k + c:54 + 4 * k + c], CV, ALU.mult, ALU.add,
                            rXL + rCV + [("VEC",)], rCV)
                    ACT(CVB, CV, AF.Copy, rCV, rCVB)
                    yield
                    ba = ga()
                    MM(PSB[ba], BD[:, (0 * 4 + c) * 128:(0 * 4 + c) * 128 + 128], CVB, True, True, rCVB + [("BD",)], rPS(ba))
                    bi_ = ga()
                    MM(PSB[bi_], BD[:, (1 * 4 + c) * 128:(1 * 4 + c) * 128 + 128], CVB, True, True, rCVB + [("BD",)], rPS(bi_))
                    ACT(RR_, PSB[ba], AF.Tanh, rPS(ba) + [("CL",)], rR, scale=0.5, bias=CL[:, 8 + c:9 + c])
                    ACT(II_, PSB[bi_], AF.Tanh, rPS(bi_) + [("CL",)], rI, scale=0.5, bias=CL[:, 12 + c:13 + c])
                    TS(RR_, RR_, 0.5, 0.5, ALU.mult, ALU.add, rR, rR)
                    TS(II_, II_, 0.5, 0.5, ALU.mult, ALU.add, rI, rI)
                    yield
                    ACT(AA_, RR_, AF.Tanh, rR + [("CL",)], rA, scale=CL[:, 4 + c:5 + c])
                    ACT(RR_, RR_, AF.Exp, rR + [("CL",)], rR, scale=CL[:, c:c + 1])
                    STT(RR_, RR_, 1.0, AA_, ALU.add, ALU.mult, rR + rA, rR)
                    TS(AA_, RR_, 1.0, None, ALU.add, None, rR, rA)
                    STT(RR_, RR_, 2.0, RR_, ALU.add, ALU.mult, rR, rR)
                    yield
                    ACT(RR_, RR_, AF.Ln, rR, rR, scale=-1.0)
                    ACT(RR_, RR_, AF.Exp, rR, rR, scale=0.5)
                    TTo(II_, II_, CV, ALU.mult, rI + rCV, rI)
                    TTo(II_, II_, RR_, ALU.mult, rI + rR, rI)
                    yield
                    P.add("dve", lambda e, c=c: e.tensor_tensor_scan(out=CV, data0=AA_, data1=II_, initial=HS[:, c:c + 1],
                                                                    op0=ALU.mult, op1=ALU.add),
                          rA + rI + [("HS", c)], rCV)
                    CP(HS[:, c:c + 1], CV[:, T - 1:T], rCV, [("HS", c)])
                    TTo(YL3[:, c, :], CV, GL, ALU.mult, rCV + rGL, [("YL", c)])
                    if c == 0:
                        dump(YL3[:, 0, :], [("YL", 0)], 1, ti)
                    yield

            nkc = 4 * (ti + 1)
            LAG = 2
            HEAT = 2

            def att_gen(h0):
                ob = 6 + h0
                for h in range(h0, 8, 2):
                    j = h // 2

                    def S_step(kc, h=h):
                        bk = ga()
                        qlo = max(0, kc * 128 - t0)
                        diag = kc * 128 >= t0
                        MM(PSB[bk][:, qlo:T], KT3[0:96, h, kc * 128:(kc + 1) * 128], QT3[0:96, h, qlo:T], True, not diag,
                           [("KTn", kc // 4), ("KTr", kc // 4)] + rQT(h), rPS(bk))
                        if diag:
                            MM(PSB[bk][:, qlo:qlo + 64], MA[0:1, :], MB[0:1, :], False, True, [("MA",), ("MB",)], rPS(bk))
                        pb = 3 * h0 + kc % 3
                        ACT(PT6[:, pb, qlo:T], PSB[bk][:, qlo:T], AF.Exp, rPS(bk), rPT(pb), scale=SCALE)

                    def PV_step(kc, h=h, j=j):
                        qlo = max(0, kc * 128 - t0)
                        pb = 3 * h0 + kc % 3
                        wo = 192 * j + (0 if h % 2 == 0 else 64)
                        last = kc == nkc - 1
                        MM(PSB[ob][:, qlo:T], VC3[:, kc, wo:wo + 128], PT6[:, pb, qlo:T], kc == 0, last and HEAT == 0,
                           [("V", kc), ("VCones",)] + rPT(pb), rPS(ob))
                        for hh in range(HEAT):
                            MM(PSB[ob][:, T - 128:T], ZEROS[:], ONES[:], False, last and hh == HEAT - 1,
                               [("ZEROS",), ("ONES",)], rPS(ob))

                    for step in range(nkc + LAG):
                        if step < nkc:
                            S_step(step)
                        if step >= LAG:
                            PV_step(step - LAG)
                        if step < nkc + LAG - 1:
                            yield
                    if h % 2 == 0:
                        o_sl, d_sl, tt, rtt = slice(0, 64), slice(64, 128), RS, rRS
                    else:
                        o_sl, d_sl, tt, rtt = slice(64, 128), slice(0, 64), RSKV, rRSKV
                    P.add("dve", lambda e, tt=tt, o_sl=o_sl, d_sl=d_sl: e.reciprocal(out=tt[o_sl, :], in_=PSB[ob][d_sl, :]), rPS(ob), rtt)
                    TTo(YM3[o_sl, j, :], PSB[ob][o_sl, :], tt[o_sl, :], ALU.mult, rPS(ob) + rtt, [("TT", j // 2)])
                    yield

            lg = lru_gen()
            agens = [att_gen(0), att_gen(1)]
            att_rounds = 4 * (nkc + LAG)
            stride = max(1, (att_rounds - 4) // 24) if LRU_SPREAD else 1
            rnd = 0
            lru_alive = True
            while agens or lru_alive:
                if lru_alive and (rnd % stride == 0 or not agens):
                    try:
                        next(lg)
                    except StopIteration:
                        lru_alive = False
                for g_ in list(agens):
                    try:
                        next(g_)
                    except StopIteration:
                        agens.remove(g_)
                rnd += 1

            dump(YM3[:, 0, :], [("TT", 0)], 5, ti)
            dump(YL3[:, 1, :], [("YL", 1)], 0, ti)
            dump(YL3[:, 2, :], [("YL", 2)], 2, ti)
            dump(YL3[:, 3, :], [("YL", 3)], 3, ti)
            dump(YM3[:, 1, :], [("TT", 0)], 4, ti)
            dump(YM3[:, 2, :], [("TT", 1)], 6, ti)
            dump(YM3[:, 3, :], [("TT", 1)], 7, ti)
            for c in range(8):
                if c % 4 == 0:
                    slot = use_block("wout")
                    Wv = W3[:, slot, :]
                bk = ga()
                if len(pending) > 1:
                    pending.pop(0)()
                for kc in range(8):
                    o = ((c % 4) * 8 + kc) * 128
                    if kc < 4:
                        MM(PSB[bk], Wv[:, o:o + 128], YL3[:, kc, :], kc == 0, kc == 7, [("W", slot), ("YL", kc)], rPS(bk))
                    else:
                        MM(PSB[bk], Wv[:, o:o + 128], YM3[:, kc - 4, :], kc == 0, kc == 7, [("W", slot), ("TT", (kc - 4) // 2)], rPS(bk))
                ACT(F3[:, c, :], PSB[bk], AF.Copy, rPS(bk), rF(c))
                sq = c % 2
                ACT(SQ[sq], PSB[bk], AF.Square, rPS(bk), rSQ(sq))
                pending.append(lambda c=c, sq=sq: MM(PSB[6], ONES[:], SQ[sq], c == 0, c == 7, rSQ(sq) + [("ONES",)], rPS(6)))
            flush_pending()
            norm_rstd(6, 1024, RS[:], rRS)
            for c in range(8):
                STT(F3[:, c, :], F3[:, c, :], GP[:, 8 + c:9 + c], RS[:], ALU.mult, ALU.mult, rF(c) + rRS + [("GP",)], rF(c))
                TTo(H3[:, c, :], H3[:, c, :], F3[:, c, :], ALU.add, [("H", c)] + rF(c), [("H", c)], eng="dve")

        rH_all = [("H", c) for c in range(8)]
        for ti in range(ntiles):
            t0 = ti * T
            for c in range(8):
                P.add("sp", lambda e, t0=t0, c=c: e.dma_start(out=H3[:, c, :], in_=xT3[:, c, t0:t0 + T]), reads=(), writes=[("H", c)], dma=("x", c))
            if ti == 0:
                ensure_loaded(1)
            if debug_stop == "ffn1":
                ffn(1, 0, 0, False, ti)
                gstate["next_use"] = (ti + 1) * len(tseq)
                gstate["next_load"] = max(gstate["next_load"], gstate["next_use"])
                store(H3[:, :, :], rH_all, ti)
                continue
            ffn(1, 0, 0, False, ti, skip_prenorm=(PREFETCH and ti > 0))
            mixer(ti)
            if debug_stop == "mixdump":
                gstate["next_use"] = (ti + 1) * len(tseq)
                gstate["next_load"] = max(gstate["next_load"], gstate["next_use"])
                continue
            if debug_stop == "mix":
                gstate["next_use"] = (ti + 1) * len(tseq)
                gstate["next_load"] = max(gstate["next_load"], gstate["next_use"])
                store(H3[:, :, :], rH_all, ti)
                continue
            ffn(2, 32, 16, True, ti, prefetch_next=(ti + 1 if (PREFETCH and ti + 1 < ntiles) else None))
            store(F3[:, :, :], rF_all, ti)
            if ti == 0:
                while pend_wb:
                    pend_wb.pop(0)()
                P.add("dve", lambda e: e.memset(VC[:, 3072:].rearrange("p (n x) -> p n x", x=192)[:, :, 64:128], 1.0),
                      reads=[("STG", 0), ("STG", 1)],
                      writes=[("VCones",), ("STG", 0), ("STG", 1)] + [("V", k) for k in range(4, 32)])
        P.add("sp", None, reads=[("OUT", ti) for ti in range(ntiles)] + dumps, writes=())

        P.finalize()
        with nc.Block() as block:
            @block.sync
            def _(e):
                P.emit("sp", e, eng_sems, dma_sems)

            @block.gpsimd
            def _(e):
                P.emit("pool", e, eng_sems, dma_sems)

            @block.tensor
            def _(e):
                P.emit("pe", e, eng_sems, dma_sems)

            @block.scalar
            def _(e):
                P.emit("act", e, eng_sems, dma_sems)

            @block.vector
            def _(e):
                P.emit("dve", e, eng_sems, dma_sems)
    return nc


def _lhsT_tiles(Wm):
    K, M = Wm.shape
    return Wm.reshape(K // 128, 128, M).transpose(1, 0, 2)


def pack_weights(inp):
    f32 = np.float32
    out = {}
    for f, pre in ((1, "w_ffn1"), (2, "w_ffn2")):
        Wg = np.asarray(inp[pre + "_gate"][0], f32)
        Wu = np.asarray(inp[pre + "_up"][0], f32)
        Wd = np.asarray(inp[pre + "_down"][0], f32)
        G = np.stack([Wg.reshape(8, 128, 22, 128), Wu.reshape(8, 128, 22, 128)], axis=0)
        G = G.reshape(2, 8, 128, 11, 2, 128)
        G = G.transpose(3, 2, 4, 0, 1, 5)
        out["gu%d" % f] = np.ascontiguousarray(G.reshape(11, 128, 4096))
        Wd4 = Wd.reshape(22, 128, 8, 128)
        for tag, (j0, nj) in zip("ab", HALVES):
            A = Wd4[j0:j0 + nj].reshape(nj, 128, 4, 2, 128)
            A = A.transpose(2, 1, 3, 0, 4)
            out["wd%d%s" % (f, tag)] = np.ascontiguousarray(A.reshape(4, 128, 2 * nj * 128))
    w_in = np.asarray(inp["w_in"][0], f32)
    cols = []
    for c in range(3):
        cols.append(w_in[:, 1024 + 128 * c:1024 + 128 * (c + 1)])
    for c in range(2):
        cols.append(w_in[:, 1408 + 128 * c:1408 + 128 * (c + 1)])
    kr = np.zeros((1024, 128), f32)
    kr[:, 64:96] = w_in[:, 1664:1696]
    kr[:, 96:112] = w_in[:, 1680:1696]
    kr[:, 112:128] = w_in[:, 1664:1680]
    cols.append(kr)
    cols.append(np.zeros((1024, 128), f32))
    cols.append(np.zeros((1024, 128), f32))
    for c in range(4):
        cols.append(w_in[:, 128 * c:128 * (c + 1)])
        cols.append(w_in[:, 512 + 128 * c:512 + 128 * (c + 1)])
    while len(cols) < 16:
        cols.append(np.zeros((1024, 128), f32))
    tiles = [_lhsT_tiles(c_).reshape(128, 1024) for c_ in cols]
    out["win"] = np.ascontiguousarray(np.stack([np.concatenate(tiles[4 * b:4 * b + 4], axis=1) for b in range(4)]))
    wq = np.asarray(inp["w_q_b"][0], f32)
    qt = []
    for h in range(8):
        m = np.zeros((384, 128), f32)
        m[:, 0:96] = wq[:, 96 * h:96 * h + 96]
        m[:, 96:112] = wq[:, 96 * h + 80:96 * h + 96]
        m[:, 112:128] = wq[:, 96 * h + 64:96 * h + 80]
        qt.append(_lhsT_tiles(m).reshape(128, 384))
    out["wqb"] = np.ascontiguousarray(np.concatenate(qt, axis=1)[None])
    wkv = np.asarray(inp["w_kv_b"][0], f32).reshape(256, 8, 128)
    kn = wkv[:, :, 0:64].reshape(256, 512)
    vv = wkv[:, :, 64:128].reshape(256, 512)
    kpart = _lhsT_tiles(kn).reshape(128, 1024)
    vpart = _lhsT_tiles(vv).reshape(128, 1024)
    out["wkvb"] = np.ascontiguousarray(np.concatenate([kpart, vpart], axis=1)[None])
    wo = np.asarray(inp["w_out"][0], f32)
    wo4 = wo.reshape(8, 128, 2, 4, 128)
    wo4 = wo4.transpose(2, 1, 3, 0, 4)
    out["wout"] = np.ascontiguousarray(wo4.reshape(2, 128, 4096))
    vec = np.zeros((128, NV), f32)

    def put(col, v):
        v = np.asarray(v, f32).reshape(-1, 128)
        vec[:, col:col + v.shape[0]] = v.T
    put(0, inp["g_ffn1_pre"][0]); put(8, inp["g_ffn1_post"][0]); put(16, inp["g_mix_pre"][0])
    put(24, inp["g_mix_post"][0]); put(32, inp["g_ffn2_pre"][0]); put(40, inp["g_ffn2_post"][0])
    put(48, inp["q_a_norm"][0]); put(51, inp["kv_a_norm"][0])
    cw = np.asarray(inp["conv_w"][0], f32)
    for k in range(4):
        put(53 + 4 * k, cw[k])
    put(69, inp["conv_b"][0]); put(73, inp["b_lru_a"][0]); put(77, inp["b_lru_x"][0]); put(81, inp["lru_lambda"][0])
    inv_freq = (1.0 / (np.float32(10000.0) ** (np.arange(0, 32, 2, dtype=np.float32) / np.float32(32)))).astype(f32)
    for r0 in (64, 80, 96, 112):
        vec[r0:r0 + 16, 85] = inv_freq
    vec[:, 86] = 1.0
    vec[96:112, 86] = -1.0
    out["vecs"] = vec
    wa = np.asarray(inp["w_lru_a"][0], f32)
    wx = np.asarray(inp["w_lru_x"][0], f32)
    bd = np.zeros((128, 2, 4, 128), f32)
    for g, wm in enumerate((wa, wx)):
        for c in range(4):
            for a in range(2):
                bd[64 * a:64 * a + 64, g, c, 64 * a:64 * a + 64] = wm[2 * c + a]
    out["bdw"] = np.ascontiguousarray(bd.reshape(128, 1024))
    return out


_NC_CACHE = {}


def kernel(**inputs):
    x = np.asarray(inputs["x"], np.float32)
    positions = np.asarray(inputs["positions"], np.int32)
    B = x.shape[0]
    packed = pack_weights(inputs)
    dbg = os.environ.get("MK_DEBUG_STOP") or None
    key = dbg
    if key not in _NC_CACHE:
        _NC_CACHE[key] = build(debug_stop=dbg)
    nc = _NC_CACHE[key]
    in_maps = []
    for b in range(B):
        m = {"xT": np.ascontiguousarray(x[b].T), "pos": np.ascontiguousarray(positions[b:b + 1])}
        m.update(packed)
        in_maps.append(m)
    res = run_bass_kernel_spmd(nc, in_maps, core_ids=list(range(B)))
    out = np.stack([np.asarray(res.results[b]["outT"]).T for b in range(B)])
    return np.ascontiguousarray(out.astype(np.float32))
```

```python
import os
import math
from contextlib import ExitStack

import numpy as np
import concourse.bass as bass
import concourse.mybir as mybir
from concourse.bass_utils import run_bass_kernel_spmd

F32 = mybir.dt.float32
BF16 = mybir.dt.bfloat16
I32 = mybir.dt.int32
AF = mybir.ActivationFunctionType
ALU = mybir.AluOpType

D = 1024
S = 4096
T = 512
NT = S // T
DFF = 2816
HALVES = [(0, 12), (12, 10)]
EPS = 1e-6
NV = 88
SCALE = 96.0 ** -0.5
MAGIC = 12582912.0
TWO_PI = 2.0 * math.pi


def _split_2pi():
    c1 = 6.28125
    r = TWO_PI - c1
    e = math.floor(math.log2(abs(r)))
    q = 2.0 ** (e - 9)
    c2 = round(r / q) * q
    c3 = float(np.float32(TWO_PI - c1 - c2))
    return c1, float(np.float32(c2)), c3


LRU_SPREAD = os.environ.get("MK_LRU_SPREAD", "1") == "1"
PREFETCH = os.environ.get("MK_PREFETCH", "1") == "1"
STRICT_SYNC = os.environ.get("MK_STRICT_SYNC", "1") == "1"
C1, C2, C3 = _split_2pi()
PI_CL = float(np.float32(3.1415925))

WSPEC = {
    "gu1": (11, 4096), "wd1a": (4, 3072), "wd1b": (4, 2560),
    "win": (4, 4096), "wqb": (1, 3072), "wkvb": (1, 2048), "wout": (2, 4096),
    "gu2": (11, 4096), "wd2a": (4, 3072), "wd2b": (4, 2560),
}


def tile_block_seq():
    seq = []
    for f in (1, 2):
        fs = []
        for p in range(6):
            fs.append(("gu%d" % f, p))
        for q in range(4):
            fs.append(("wd%da" % f, q))
        for p in range(6, 11):
            fs.append(("gu%d" % f, p))
        for q in range(4):
            fs.append(("wd%db" % f, q))
        if f == 1:
            seq += fs
            seq.append(("win", 0))
            seq.append(("win", 1))
            seq.append(("wqb", 0))
            seq.append(("wkvb", 0))
            seq.append(("win", 2))
            seq.append(("win", 3))
            for b in range(2):
                seq.append(("wout", b))
        else:
            seq += fs
    return seq


class Prog:
    def __init__(self):
        self.ops = []
        self.lastw = {}
        self.readers = {}
        self.dma_cnt = {}
        self.epos = {}

    def add(self, eng, fn, reads=(), writes=(), dma=None):
        idx = len(self.ops)
        deps = set()
        raw = set()
        for r in reads:
            w = self.lastw.get(r)
            if w is not None:
                deps.add(w)
                raw.add(w)
        for r in writes:
            w = self.lastw.get(r)
            if w is not None:
                deps.add(w)
            rl = self.readers.get(r)
            if rl:
                deps.update(rl)
        deps.discard(idx)
        for r in reads:
            self.readers.setdefault(r, []).append(idx)
        for r in writes:
            self.lastw[r] = idx
            self.readers[r] = []
        pos = self.epos.get(eng, 0)
        self.epos[eng] = pos + 1
        raw_same = set()
        if eng in ("act", "dve", "pool") and dma is None:
            for m in (deps if STRICT_SYNC else raw):
                om = self.ops[m]
                if om["eng"] == eng and om["dma"] is None and (STRICT_SYNC or pos - om["pos"] <= 2):
                    raw_same.add(m)
        op = {"eng": eng, "fn": fn, "deps": deps, "dma": dma, "sig": False, "pos": pos, "raw_same": raw_same}
        if dma is not None:
            self.dma_cnt[dma] = self.dma_cnt.get(dma, 0) + 16
            op["dcnt"] = self.dma_cnt[dma]
        self.ops.append(op)
        return idx

    def finalize(self):
        ops = self.ops
        for op in ops:
            for m in op["deps"]:
                om = ops[m]
                if om["dma"] is None and (om["eng"] != op["eng"] or m in op["raw_same"]):
                    om["sig"] = True
        cnt = {}
        for op in ops:
            if op["dma"] is None and op["sig"]:
                e = op["eng"]
                cnt[e] = cnt.get(e, 0) + 1
                op["scnt"] = cnt[e]

    def emit(self, eng_name, e, eng_sems, dma_sems):
        ops = self.ops
        known = {}
        for op in ops:
            if op["eng"] != eng_name:
                continue
            waits = {}
            for m in op["deps"]:
                om = ops[m]
                if om["dma"] is not None:
                    key = ("d", om["dma"])
                    val = om["dcnt"]
                elif om["eng"] != eng_name or m in op["raw_same"]:
                    key = ("e", om["eng"])
                    val = om["scnt"]
                else:
                    continue
                if val > waits.get(key, 0):
                    waits[key] = val
            for key, val in waits.items():
                if known.get(key, 0) < val:
                    sem = dma_sems[key[1]] if key[0] == "d" else eng_sems[key[1]]
                    e.wait_ge(sem, val)
                    known[key] = val
            if op["fn"] is not None:
                ins = op["fn"](e)
                if op["dma"] is not None:
                    ins.then_inc(dma_sems[op["dma"]], 16)
                elif op["sig"]:
                    ins.then_inc(eng_sems[eng_name], 1)


def build(debug_stop=None, ntiles=NT):
    nc = bass.Bass("TRN2", target_bir_lowering=False)
    P = Prog()

    xT = nc.dram_tensor("xT", [D, S], F32, kind="ExternalInput").ap()
    pos = nc.dram_tensor("pos", [1, S], I32, kind="ExternalInput").ap()
    vecs = nc.dram_tensor("vecs", [128, NV], F32, kind="ExternalInput").ap()
    bdw = nc.dram_tensor("bdw", [128, 1024], F32, kind="ExternalInput").ap()
    outT = nc.dram_tensor("outT", [D, S], F32, kind="ExternalOutput").ap()
    wf = {}
    wb = {}
    for name, (nb, fr) in WSPEC.items():
        wf[name] = nc.dram_tensor(name, [nb, 128, fr], F32, kind="ExternalInput").ap()
        wb[name] = nc.dram_tensor(name + "_bf", [nb, 128, fr], BF16, kind="Internal").ap()
    xT3 = xT.rearrange("(c p) t -> p c t", p=128)
    outT3 = outT.rearrange("(c p) t -> p c t", p=128)

    tseq = tile_block_seq()
    blocks = []
    for i in range(ntiles):
        blocks += tseq

    dma_keys = [("w", s) for s in range(3)] + [("cv", k) for k in range(8)] + [("x", c) for c in range(8)] + [("xs", 0), ("xs", 1), ("stg", 0), ("stg", 1), ("wbk", 0), ("wbk", 1), ("wbk", 2), "pos", "st", "c0", "c1", "dbg"]

    with ExitStack() as es:
        def sb(name, shape, dt):
            return es.enter_context(nc.sbuf_tensor(name, shape, dt))

        KT = sb("KT", [128, 8 * S], BF16)
        VC = sb("VC", [128, 32 * 768], BF16)
        H = sb("H", [128, 8 * T], F32)
        XN = sb("XN", [128, 8 * T], BF16)
        HIDa = sb("HID", [128, 12 * T], BF16)
        Fa = sb("F", [128, 8 * T], F32)
        W = sb("W", [128, 3 * 4096], BF16)
        RS = sb("RS", [128, T], F32)
        SQa = sb("SQ", [128, 2 * T], BF16)
        SGa = sb("SG", [128, 2 * T], F32)
        TTa = sb("TT", [128, 2 * T], F32)
        VEC = sb("VEC", [128, NV], F32)
        ONES = sb("ONES", [128, 128], BF16)
        ZEROS = sb("ZEROS", [128, 128], BF16)
        BD = sb("BD", [128, 1024], BF16)
        MA = sb("MA", [1, 128], BF16)
        MB = sb("MB", [1, 64], BF16)
        HALO = sb("HALO", [128, 12], F32)
        HS = sb("HS", [128, 4], F32)
        CL = sb("CL", [128, 16], F32)
        GP = sb("GP", [128, 24], F32)
        LT = sb("LT", [128, 24], F32)
        YL = sb("YL", [128, 4 * T], BF16)
        PSB = [es.enter_context(nc.psum_tensor("ps%d" % b, [128, 512], F32))[:] for b in range(8)]

        eng_sems = {k: es.enter_context(nc.semaphore("s_" + k)) for k in ("pe", "act", "dve", "pool", "sp")}
        dma_sems = {}
        for k in dma_keys:
            nm = "d_" + (k if isinstance(k, str) else "%s%d" % k)
            dma_sems[k] = es.enter_context(nc.semaphore(nm))

        KT3 = KT[:].rearrange("p (h t) -> p h t", h=8)
        VC3 = VC[:].rearrange("p (c x) -> p c x", x=768)
        H3 = H[:].rearrange("p (c t) -> p c t", c=8)
        XN3 = XN[:].rearrange("p (c t) -> p c t", c=8)
        HID3 = HIDa[:].rearrange("p (c t) -> p c t", c=12)
        F3 = Fa[:].rearrange("p (c t) -> p c t", c=8)
        W3 = W[:].rearrange("p (s x) -> p s x", s=3)
        YL3 = YL[:].rearrange("p (c t) -> p c t", c=4)
        SQ = [SQa[:, 0:T], SQa[:, T:2 * T]]
        SG = [SGa[:, 0:T], SGa[:, T:2 * T]]
        QT3 = Fa[:, 0:2048].bitcast(BF16).rearrange("p (h t) -> p h t", h=8)
        PT3 = Fa[:, 2048:2816].bitcast(BF16).rearrange("p (b t) -> p b t", b=3)
        PT6 = Fa[:, 2048:3584].bitcast(BF16).rearrange("p (b t) -> p b t", b=6)
        QN3 = Fa[:, 2816:3584].bitcast(BF16).rearrange("p (c t) -> p c t", c=3)
        KVN3 = Fa[:, 3584:4096].bitcast(BF16).rearrange("p (c t) -> p c t", c=2)
        XL = HIDa[:, 0:1536].bitcast(F32)
        CV = HIDa[:, 1536:2560].bitcast(F32)
        CVB = HIDa[:, 2560:3072]
        RR_ = HIDa[:, 3072:4096].bitcast(F32)
        II_ = HIDa[:, 4096:5120].bitcast(F32)
        AA_ = HIDa[:, 5120:6144].bitcast(F32)
        PIv = HIDa[:, 3072:4096].bitcast(I32)
        GL = SGa[:, 0:T]
        TBL = TTa[:, 0:T]
        RSKV = SQa[:].bitcast(F32)
        YM3 = TTa[:].bitcast(BF16).rearrange("p (c t) -> p c t", c=4)
        T1 = TTa[:, 0:T]
        T2 = TTa[:, T:2 * T]

        def rF(c):
            return [("F", 2 * c), ("F", 2 * c + 1)]
        rF_all = [("F", g) for g in range(16)]
        rQT = lambda h: [("F", h)]
        rPT = lambda b: [("F", 8 + b)]
        rQN = lambda c: [("F", 11 + c)]
        rKVN = lambda c: [("F", 14 + c)]
        rHID = lambda j: [("HID", j)]
        rXLh = []
        rXL = [("HID", 0), ("HID", 1), ("HID", 2)]
        rCV = [("HID", 3), ("HID", 4)]
        rCVB = [("HID", 5)]
        rR = [("HID", 6), ("HID", 7)]
        rI = [("HID", 8), ("HID", 9)]
        rA = [("HID", 10), ("HID", 11)]
        rSG = lambda b: [("SG", 2 * b), ("SG", 2 * b + 1)]
        rGL = rSG(0)
        rTBL = [("TT", 0)]
        rSQ = lambda b: [("SQ", b)]
        rRSKV = [("SQ", 0), ("SQ", 1)]
        rRS = [("RS",)]
        rT1 = [("TT", 0)]
        rT2 = [("TT", 1)]
        rPS = lambda b: [("PS", b)]

        def rW(slot):
            return [("W", slot), ("Wp", slot, 0), ("Wp", slot, 1), ("Wp", slot, 2)]

        def MM(out, lhsT, rhs, start, stop, reads, writes):
            rr = []
            for r in reads:
                if len(r) == 2 and r[0] == "W":
                    rr += rW(r[1])
                else:
                    rr.append(r)
            P.add("pe", lambda e: e.matmul(out, lhsT=lhsT, rhs=rhs, start=start, stop=stop), rr, writes)

        def ACT(out, in_, func, reads, writes, scale=None, bias=None):
            kw = {}
            if scale is not None:
                kw["scale"] = scale
            if bias is not None:
                kw["bias"] = bias
            P.add("act", lambda e: e.activation(out=out, in_=in_, func=func, **kw), reads, writes)

        def TTo(out, in0, in1, op, reads, writes, eng="dve"):
            P.add(eng, lambda e: e.tensor_tensor(out=out, in0=in0, in1=in1, op=op), reads, writes)

        def TS(out, in0, s1, s2, op0, op1, reads, writes, eng="dve"):
            if s2 is None:
                P.add(eng, lambda e: e.tensor_scalar(out=out, in0=in0, scalar1=s1, scalar2=None, op0=op0), reads, writes)
            else:
                P.add(eng, lambda e: e.tensor_scalar(out=out, in0=in0, scalar1=s1, scalar2=s2, op0=op0, op1=op1), reads, writes)

        def STT(out, in0, scalar, in1, op0, op1, reads, writes):
            P.add("dve", lambda e: e.scalar_tensor_tensor(out=out, in0=in0, scalar=scalar, in1=in1, op0=op0, op1=op1), reads, writes)

        def CP(out, in_, reads, writes, eng="dve"):
            P.add(eng, lambda e: e.tensor_copy(out=out, in_=in_), reads, writes)

        def MS(ap, val, writes, eng="dve"):
            P.add(eng, lambda e: e.memset(ap, val), (), writes)

        gstate = {"ga": 0, "next_load": 0, "next_use": 0}

        def ga():
            b = gstate["ga"]
            gstate["ga"] = (b + 1) % 6
            return b

        STG = [VC[:, 3072:11264].bitcast(F32), VC[:, 11264:19456].bitcast(F32)]
        pend_wb = []

        def ensure_loaded(upto):
            while gstate["next_load"] <= min(upto, len(blocks) - 1):
                n = gstate["next_load"]
                name, bi = blocks[n]
                slot = n % 3
                fr = WSPEC[name][1]
                if n < len(tseq):
                    sb_ = n % 2
                    P.add("sp", lambda e, sb_=sb_, fr=fr, name=name, bi=bi: e.dma_start(out=STG[sb_][:, 0:fr], in_=wf[name][bi, :, :]),
                          reads=(), writes=[("STG", sb_)], dma=("stg", sb_))
                    while pend_wb:
                        pend_wb.pop(0)()
                    q2 = (fr * 9 // 16) // 64 * 64
                    P.add("dve", lambda e, sb_=sb_, slot=slot, q2=q2: e.tensor_copy(out=W3[:, slot, 0:q2], in_=STG[sb_][:, 0:q2]),
                          reads=[("STG", sb_)], writes=[("Wp", slot, 0), ("Wp", slot, 1)])
                    P.add("act", lambda e, sb_=sb_, slot=slot, q2=q2, fr=fr: e.activation(out=W3[:, slot, q2:fr], in_=STG[sb_][:, q2:fr], func=AF.Copy),
                          reads=[("STG", sb_)], writes=[("Wp", slot, 2)])
                    pend_wb.append(lambda slot=slot, fr=fr, name=name, bi=bi: P.add(
                        "sp", lambda e: e.dma_start(out=wb[name][bi, :, :], in_=W3[:, slot, 0:fr]),
                        reads=rW(slot), writes=[("WB", name, bi)], dma=("wbk", slot)))
                else:
                    while pend_wb:
                        pend_wb.pop(0)()
                    P.add("sp", lambda e, slot=slot, fr=fr, name=name, bi=bi: e.dma_start(out=W3[:, slot, 0:fr], in_=wb[name][bi, :, :]),
                          reads=[("WB", name, bi)], writes=rW(slot), dma=("w", slot))
                gstate["next_load"] += 1

        def use_block(expect):
            n = gstate["next_use"]
            gstate["next_use"] += 1
            assert blocks[n][0] == expect, (blocks[n], expect)
            ensure_loaded(n + 2)
            return n % 3

        P.add("sp", lambda e: e.dma_start(out=VEC[:], in_=vecs[:, :]), (), [("VEC",)], dma="c0")
        P.add("pool", lambda e: e.dma_start(out=BD[:], in_=bdw[:, :]), (), [("BD",)], dma="c1")
        MS(ONES[:], 1.0, [("ONES",)])
        MS(ZEROS[:], 0.0, [("ZEROS",)])
        MS(VC[:, 0:3072].rearrange("p (n x) -> p n x", x=192)[:, :, 64:128], 1.0, [("VCones",)])
        MS(MA[0:1, 0:64], 0.0, [("MA",)])
        MS(MA[0:1, 64:128], 1.0, [("MA",)])
        MS(MB[0:1, :], -30000.0, [("MB",)])
        MS(HALO[:], 0.0, [("HALO", c) for c in range(4)])
        MS(HS[:], 0.0, [("HS", c) for c in range(4)])
        TS(GP[:, 0:8], VEC[:, 8:16], 0.5, None, ALU.mult, None, [("VEC",)], [("GP",)])
        TS(GP[:, 8:16], VEC[:, 24:32], 1.0, None, ALU.mult, None, [("VEC",)], [("GP",)])
        TS(GP[:, 16:24], VEC[:, 40:48], 0.5, None, ALU.mult, None, [("VEC",)], [("GP",)])
        rLT = [("LT",)]
        Y_ = LT[:, 0:4]; E_ = LT[:, 4:8]; Z_ = LT[:, 8:12]; Z2_ = LT[:, 12:16]; P_ = LT[:, 16:20]; Q_ = LT[:, 20:24]
        TS(Y_, VEC[:, 81:85], -1.0, None, ALU.mult, None, [("VEC",)], rLT)
        TTo(Q_, Y_, VEC[:, 81:85], ALU.max, rLT + [("VEC",)], rLT)
        ACT(E_, Q_, AF.Exp, rLT, rLT, scale=-1.0)
        TS(Q_, E_, 2.0, None, ALU.add, None, rLT, rLT)
        P.add("dve", lambda e: e.reciprocal(out=Q_, in_=Q_), rLT, rLT)
        TTo(Z_, E_, Q_, ALU.mult, rLT, rLT)
        TTo(Z2_, Z_, Z_, ALU.mult, rLT, rLT)
        MS(P_, 1.0 / 13.0, rLT)
        for kk in (11, 9, 7, 5, 3, 1):
            TTo(P_, P_, Z2_, ALU.mult, rLT, rLT)
            TS(P_, P_, 1.0 / kk, None, ALU.add, None, rLT, rLT)
        TTo(P_, P_, Z_, ALU.mult, rLT, rLT)
        TS(Q_, Y_, 0.0, None, ALU.max, None, rLT, rLT)
        STT(P_, P_, 2.0, Q_, ALU.mult, ALU.add, rLT, rLT)
        TS(CL[:, 0:4], P_, -8.0, None, ALU.mult, None, rLT, [("CL",)])
        TS(CL[:, 4:8], P_, -4.0, None, ALU.mult, None, rLT, [("CL",)])
        TS(CL[:, 8:12], VEC[:, 73:77], 0.5, None, ALU.mult, None, [("VEC",)], [("CL",)])
        TS(CL[:, 12:16], VEC[:, 77:81], 0.5, None, ALU.mult, None, [("VEC",)], [("CL",)])

        def norm_rstd(ssb, n, dst, rdst):
            ACT(dst, PSB[ssb], AF.Ln, rPS(ssb), rdst, scale=1.0 / n, bias=EPS)
            ACT(dst, dst, AF.Exp, rdst, rdst, scale=-0.5)

        def prenorm(gcol):
            ss = ga()
            for c in range(8):
                q = c % 2
                ACT(SQ[q], H3[:, c, :], AF.Square, [("H", c)], rSQ(q))
                MM(PSB[ss], ONES[:], SQ[q], c == 0, c == 7, rSQ(q) + [("ONES",)], rPS(ss))
            norm_rstd(ss, 1024, RS[:], rRS)
            for c in range(8):
                STT(XN3[:, c, :], H3[:, c, :], VEC[:, gcol + c:gcol + c + 1], RS[:], ALU.mult, ALU.mult,
                    [("H", c), ("VEC",)] + rRS, [("XN", c)])

        def pf_pass1(ti_next, c):
            tn = ti_next * T
            stg, rstg, key = (T1, rT1, ("xs", 0)) if c % 2 == 0 else (T2, rT2, ("xs", 1))
            P.add("sp", lambda e: e.dma_start(out=stg, in_=xT3[:, c, tn:tn + T]), reads=(), writes=rstg, dma=key)
            sq = c % 2
            ACT(SQ[sq], stg, AF.Square, rstg, rSQ(sq))
            MM(PSB[7], ONES[:], SQ[sq], c == 0, c == 7, rSQ(sq) + [("ONES",)], rPS(7))

        def pf_rstd():
            norm_rstd(7, 1024, T1, rT1)

        def pf_pass2(ti_next, c):
            tn = ti_next * T
            stg, rstg, key = (T2, rT2, ("xs", 1)) if c % 2 == 0 else (SG[0], rSG(0), ("xs", 0))
            P.add("sp", lambda e: e.dma_start(out=stg, in_=xT3[:, c, tn:tn + T]), reads=(), writes=rstg, dma=key)
            STT(XN3[:, c, :], stg, VEC[:, c:c + 1], T1, ALU.mult, ALU.mult, rstg + [("VEC",)] + rT1, [("XN", c)])

        def ffn(f, gpre_col, gp_col, final, ti, skip_prenorm=False, prefetch_next=None):
            t0 = ti * T
            fpend = []
            if not skip_prenorm:
                prenorm(gpre_col)
            for hf, (j0, nj) in enumerate(HALVES):
                for p in range(nj // 2):
                    slot = use_block("gu%d" % f)
                    Wv = W3[:, slot, :]
                    for jj in range(2):
                        jl = 2 * p + jj
                        bg = ga()
                        for kc in range(8):
                            o = ((jj * 2 + 0) * 8 + kc) * 128
                            MM(PSB[bg], Wv[:, o:o + 128], XN3[:, kc, :], kc == 0, kc == 7, [("W", slot), ("XN", kc)], rPS(bg))
                        bu = ga()
                        for kc in range(8):
                            o = ((jj * 2 + 1) * 8 + kc) * 128
                            MM(PSB[bu], Wv[:, o:o + 128], XN3[:, kc, :], kc == 0, kc == 7, [("W", slot), ("XN", kc)], rPS(bu))
                        sg = jl % 2
                        ACT(SG[sg], PSB[bg], AF.Silu, rPS(bg), rSG(sg))
                        TTo(HID3[:, jl, :], SG[sg], PSB[bu], ALU.mult, rSG(sg) + rPS(bu), rHID(jl))
                    if f == 1 and hf == 0 and p == 1 and debug_stop != "ffn1":
                        rope_tables(ti)
                    if hf == 1 and prefetch_next is not None:
                        for c_ in ((0, 1), (2, 3), (4, 5), (6,), (7,))[p]:
                            pf_pass1(prefetch_next, c_)
                if hf == 1 and prefetch_next is not None:
                    pf_rstd()
                for q in range(4):
                    slot = use_block("wd%d%s" % (f, "ab"[hf]))
                    Wv = W3[:, slot, :]
                    for cc in range(2):
                        c = 2 * q + cc
                        bk = ga()
                        if len(fpend) > 1:
                            fpend.pop(0)()
                        for jl in range(nj):
                            o = (cc * nj + jl) * 128
                            MM(PSB[bk], Wv[:, o:o + 128], HID3[:, jl, :], jl == 0, jl == nj - 1, [("W", slot)] + rHID(jl), rPS(bk))
                        if hf == 0:
                            ACT(F3[:, c, :], PSB[bk], AF.Copy, rPS(bk), rF(c))
                        else:
                            TTo(F3[:, c, :], F3[:, c, :], PSB[bk], ALU.add, rF(c) + rPS(bk), rF(c))
                            sq = c % 2
                            ACT(SQ[sq], F3[:, c, :], AF.Square, rF(c), rSQ(sq))
                            fpend.append(lambda c=c, sq=sq: MM(PSB[6], ONES[:], SQ[sq], c == 0, c == 7, rSQ(sq) + [("ONES",)], rPS(6)))
                            if prefetch_next is not None:
                                pf_pass2(prefetch_next, c)
            while fpend:
                fpend.pop(0)()
            norm_rstd(6, 1024, RS[:], rRS)
            for c in range(8):
                STT(F3[:, c, :], F3[:, c, :], GP[:, gp_col + c:gp_col + c + 1], RS[:], ALU.mult, ALU.mult,
                    rF(c) + rRS + [("GP",)], rF(c))
                aeng = "dve"
                if not final:
                    TTo(H3[:, c, :], H3[:, c, :], F3[:, c, :], ALU.add, [("H", c)] + rF(c), [("H", c)], eng=aeng)
                else:
                    TTo(F3[:, c, :], H3[:, c, :], F3[:, c, :], ALU.add, [("H", c)] + rF(c), rF(c), eng=aeng)

        def store(src3, rsrc, ti):
            t0 = ti * T
            P.add("sp", lambda e: e.dma_start(out=outT3[:, :, t0:t0 + T], in_=src3), reads=rsrc, writes=[("OUT", ti)], dma="st")

        dumps = []

        def dump(ap, reads, c, ti, psl=slice(0, 128)):
            if debug_stop != "mixdump":
                return
            t0_ = ti * T
            dumps.append(("OUTD", ti, c))
            P.add("pool", lambda e: e.dma_start(out=outT3[psl, c, t0_:t0_ + T], in_=ap), reads=reads, writes=[("OUTD", ti, c)], dma="dbg")

        def rope_tables(ti):
            t0 = ti * T
            T1i = TTa[64:128, 0:T].bitcast(I32)
            X = T1[64:128, :]
            K = T2[64:128, :]
            P.add("sp", lambda e: e.dma_start(out=T1i, in_=pos[0:1, t0:t0 + T].partition_broadcast(64)),
                  reads=(), writes=rT1, dma="pos")
            CP(X, T1i, rT1, rT1)
            TS(X, X, VEC[64:128, 85:86], None, ALU.mult, None, rT1 + [("VEC",)], rT1)
            TS(K, X, 1.0 / TWO_PI, MAGIC, ALU.mult, ALU.add, rT1, rT2)
            TS(K, K, MAGIC, None, ALU.subtract, None, rT2, rT2)
            STT(X, K, -C1, X, ALU.mult, ALU.add, rT1 + rT2, rT1)
            STT(X, K, -C2, X, ALU.mult, ALU.add, rT1 + rT2, rT1)
            STT(X, K, -C3, X, ALU.mult, ALU.add, rT1 + rT2, rT1)
            TS(X, X, PI_CL, -PI_CL, ALU.min, ALU.max, rT1, rT1)
            ACT(T2[64:96, :], T1[64:96, :], AF.Sin, rT1, rT2, scale=0.5)
            ACT(T1[96:128, :], T1[96:128, :], AF.Sin, rT1 + [("VEC",)], rT1, scale=VEC[96:128, 86:87])
            TTo(T2[64:96, :], T2[64:96, :], T2[64:96, :], ALU.mult, rT2, rT2)
            TS(T1[64:96, :], T2[64:96, :], -2.0, 1.0, ALU.mult, ALU.add, rT2, rT1)

        def mixer(ti):
            t0 = ti * T
            prenorm(16)

            pending = []

            def flush_pending():
                while pending:
                    pending.pop(0)()

            def win_mm(oc):
                bk = ga()
                for kc in range(8):
                    o = ((oc % 4) * 8 + kc) * 128
                    MM(PSB[bk], win_state["Wv"][:, o:o + 128], XN3[:, kc, :], kc == 0, kc == 7,
                       [("W", win_state["slot"]), ("XN", kc)], rPS(bk))
                return bk

            win_state = {"slot": None, "Wv": None}

            def win_block():
                win_state["slot"] = use_block("win")
                win_state["Wv"] = W3[:, win_state["slot"], :]

            win_block()
            for ci in range(3):
                bk = win_mm(ci)
                flush_pending()
                ACT(QN3[:, ci, :], PSB[bk], AF.Copy, rPS(bk) + [("VEC",)], rQN(ci), scale=VEC[:, 48 + ci:49 + ci])
                sq = ci % 2
                ACT(SQ[sq], PSB[bk], AF.Square, rPS(bk), rSQ(sq))
                pending.append(lambda ci=ci, sq=sq: MM(PSB[6], ONES[:], SQ[sq], ci == 0, ci == 2, rSQ(sq) + [("ONES",)], rPS(6)))
            bk = win_mm(3)
            flush_pending()
            norm_rstd(6, 384, RS[:], rRS)
            for c in range(3):
                TTo(QN3[:, c, :], QN3[:, c, :], RS[:], ALU.mult, rQN(c) + rRS, rQN(c))
            for ci in range(2):
                if ci == 1:
                    win_block()
                    bk = win_mm(4)
                    flush_pending()
                ACT(KVN3[:, ci, :], PSB[bk], AF.Copy, rPS(bk) + [("VEC",)], rKVN(ci), scale=VEC[:, 51 + ci:52 + ci])
                sq = ci % 2
                ACT(SQ[sq], PSB[bk], AF.Square, rPS(bk), rSQ(sq))
                pending.append(lambda ci=ci, sq=sq: MM(PSB[7], ONES[:], SQ[sq], ci == 0, ci == 1, rSQ(sq) + [("ONES",)], rPS(7)))
            bk = win_mm(5)
            flush_pending()
            norm_rstd(7, 256, RSKV, rRSKV)
            for c in range(2):
                TTo(KVN3[:, c, :], KVN3[:, c, :], RSKV, ALU.mult, rKVN(c) + rRSKV, rKVN(c))
            TTo(CV[64:96, :], PSB[bk][64:96, :], TBL[64:96, :], ALU.mult, rPS(bk) + rTBL, rCV)
            TTo(RR_[64:96, :], PSB[bk][96:128, :], TBL[96:128, :], ALU.mult, rPS(bk) + rTBL, rR)
            reng = "dve"
            for h in range(8):
                TTo(KT3[64:96, h, t0:t0 + T], CV[64:96, :], RR_[64:96, :], ALU.add, rCV + rR, [("KTr", ti)], eng=reng)

            slot = use_block("wqb")
            Wv = W3[:, slot, :]
            for h in range(8):
                bk = ga()
                for kc in range(3):
                    o = (h * 3 + kc) * 128
                    MM(PSB[bk], Wv[:, o:o + 128], QN3[:, kc, :], kc == 0, kc == 2, [("W", slot)] + rQN(kc), rPS(bk))
                ACT(QT3[0:64, h, :], PSB[bk][0:64, :], AF.Copy, rPS(bk), rQT(h))
                if h % 2 == 0:
                    ta, tb, rta, rtb = CV, RR_, rCV, rR
                else:
                    ta, tb, rta, rtb = II_, AA_, rI, rA
                TTo(ta[64:96, :], PSB[bk][64:96, :], TBL[64:96, :], ALU.mult, rPS(bk) + rTBL, rta)
                TTo(tb[64:96, :], PSB[bk][96:128, :], TBL[96:128, :], ALU.mult, rPS(bk) + rTBL, rtb)
                TTo(QT3[64:96, h, :], ta[64:96, :], tb[64:96, :], ALU.add, rta + rtb, rQT(h), eng=reng)
            slot = use_block("wkvb")
            Wv = W3[:, slot, :]
            for j in range(4):
                bk = ga()
                for kc in range(2):
                    o = (kc * 4 + j) * 128
                    MM(PSB[bk], Wv[:, o:o + 128], KVN3[:, kc, :], kc == 0, kc == 1, [("W", slot)] + rKVN(kc), rPS(bk))
                ACT(KT3[0:64, 2 * j, t0:t0 + T], PSB[bk][0:64, :], AF.Copy, rPS(bk), [("KTn", ti)])
                ACT(KT3[0:64, 2 * j + 1, t0:t0 + T], PSB[bk][64:128, :], AF.Copy, rPS(bk), [("KTn", ti)])
            for s_ in range(4):
                bk = ga()
                for kc in range(2):
                    MM(PSB[bk], KVN3[:, kc, s_ * 128:(s_ + 1) * 128], Wv[:, 1024 + kc * 512:1024 + (kc + 1) * 512],
                       kc == 0, kc == 1, [("W", slot)] + rKVN(kc), rPS(bk))
                kcg = 4 * ti + s_
                vdst = VC3[:, kcg, :].rearrange("p (j x) -> p j x", x=192)
                vsrc = PSB[bk].rearrange("p (j e d) -> p j e d", j=4, e=2)
                CP(vdst[:, :, 0:64], vsrc[:, :, 0, :], rPS(bk), [("V", kcg)])
                ACT(vdst[:, :, 128:192], vsrc[:, :, 1, :], AF.Copy, rPS(bk), [("V", kcg)])


            def lru_gen():
                for c in range(4):
                    ocx = 8 + 2 * c
                    if ocx % 4 == 0:
                        win_block()
                    bkx = win_mm(ocx)
                    bkg = win_mm(ocx + 1)
                    CP(XL[:, 0:3], HALO[:, 3 * c:3 * c + 3], [("HALO", c)], rXL)
                    ACT(XL[:, 3:3 + T], PSB[bkx], AF.Copy, rPS(bkx), rXL)
                    CP(HALO[:, 3 * c:3 * c + 3], XL[:, T:T + 3], rXL, [("HALO", c)])
                    ACT(GL, PSB[bkg], AF.Gelu_apprx_tanh, rPS(bkg), rGL)
                    yield
                    TS(CV, XL[:, 0:T], VEC[:, 53 + c:54 + c], VEC[:, 69 + c:70 + c], ALU.mult, ALU.add,
                       rXL + [("VEC",)], rCV)
                    for k in range(1, 4):
                        STT(CV, XL[:, k:k + T], VEC[:, 53 + 4 * k + c:54 + 4 * k + c], CV, ALU.mult, ALU.add,
                            rXL + rCV + [("VEC",)], rCV)
                    ACT(CVB, CV, AF.Copy, rCV, rCVB)
                    yield
                    ba = ga()
                    MM(PSB[ba], BD[:, (0 * 4 + c) * 128:(0 * 4 + c) * 128 + 128], CVB, True, True, rCVB + [("BD",)], rPS(ba))
                    bi_ = ga()
                    MM(PSB[bi_], BD[:, (1 * 4 + c) * 128:(1 * 4 + c) * 128 + 128], CVB, True, True, rCVB + [("BD",)], rPS(bi_))
                    ACT(RR_, PSB[ba], AF.Tanh, rPS(ba) + [("CL",)], rR, scale=0.5, bias=CL[:, 8 + c:9 + c])
                    ACT(II_, PSB[bi_], AF.Tanh, rPS(bi_) + [("CL",)], rI, scale=0.5, bias=CL[:, 12 + c:13 + c])
                    TS(RR_, RR_, 0.5, 0.5, ALU.mult, ALU.add, rR, rR)
                    TS(II_, II_, 0.5, 0.5, ALU.mult, ALU.add, rI, rI)
                    yield
                    ACT(AA_, RR_, AF.Tanh, rR + [("CL",)], rA, scale=CL[:, 4 + c:5 + c])
                    ACT(RR_, RR_, AF.Exp, rR + [("CL",)], rR, scale=CL[:, c:c + 1])
                    STT(RR_, RR_, 1.0, AA_, ALU.add, ALU.mult, rR + rA, rR)
                    TS(AA_, RR_, 1.0, None, ALU.add, None, rR, rA)
                    STT(RR_, RR_, 2.0, RR_, ALU.add, ALU.mult, rR, rR)
                    yield
                    ACT(RR_, RR_, AF.Ln, rR, rR, scale=-1.0)
                    ACT(RR_, RR_, AF.Exp, rR, rR, scale=0.5)
                    TTo(II_, II_, CV, ALU.mult, rI + rCV, rI)
                    TTo(II_, II_, RR_, ALU.mult, rI + rR, rI)
                    yield
                    P.add("dve", lambda e, c=c: e.tensor_tensor_scan(out=CV, data0=AA_, data1=II_, initial=HS[:, c:c + 1],
                                                                    op0=ALU.mult, op1=ALU.add),
                          rA + rI + [("HS", c)], rCV)
                    CP(HS[:, c:c + 1], CV[:, T - 1:T], rCV, [("HS", c)])
                    TTo(YL3[:, c, :], CV, GL, ALU.mult, rCV + rGL, [("YL", c)])
                    if c == 0:
                        dump(YL3[:, 0, :], [("YL", 0)], 1, ti)
                    yield

            nkc = 4 * (ti + 1)
            LAG = 2
            HEAT = 2

            def att_gen(h0):
                ob = 6 + h0
                for h in range(h0, 8, 2):
                    j = h // 2

                    def S_step(kc, h=h):
                        bk = ga()
                        qlo = max(0, kc * 128 - t0)
                        diag = kc * 128 >= t0
                        MM(PSB[bk][:, qlo:T], KT3[0:96, h, kc * 128:(kc + 1) * 128], QT3[0:96, h, qlo:T], True, not diag,
                           [("KTn", kc // 4), ("KTr", kc // 4)] + rQT(h), rPS(bk))
                        if diag:
                            MM(PSB[bk][:, qlo:qlo + 64], MA[0:1, :], MB[0:1, :], False, True, [("MA",), ("MB",)], rPS(bk))
                        pb = 3 * h0 + kc % 3
                        ACT(PT6[:, pb, qlo:T], PSB[bk][:, qlo:T], AF.Exp, rPS(bk), rPT(pb), scale=SCALE)

                    def PV_step(kc, h=h, j=j):
                        qlo = max(0, kc * 128 - t0)
                        pb = 3 * h0 + kc % 3
                        wo = 192 * j + (0 if h % 2 == 0 else 64)
                        last = kc == nkc - 1
                        MM(PSB[ob][:, qlo:T], VC3[:, kc, wo:wo + 128], PT6[:, pb, qlo:T], kc == 0, last and HEAT == 0,
                           [("V", kc), ("VCones",)] + rPT(pb), rPS(ob))
                        for hh in range(HEAT):
                            MM(PSB[ob][:, T - 128:T], ZEROS[:], ONES[:], False, last and hh == HEAT - 1,
                               [("ZEROS",), ("ONES",)], rPS(ob))

                    for step in range(nkc + LAG):
                        if step < nkc:
                            S_step(step)
                        if step >= LAG:
                            PV_step(step - LAG)
                        if step < nkc + LAG - 1:
                            yield
                    if h % 2 == 0:
                        o_sl, d_sl, tt, rtt = slice(0, 64), slice(64, 128), RS, rRS
                    else:
                        o_sl, d_sl, tt, rtt = slice(64, 128), slice(0, 64), RSKV, rRSKV
                    P.add("dve", lambda e, tt=tt, o_sl=o_sl, d_sl=d_sl: e.reciprocal(out=tt[o_sl, :], in_=PSB[ob][d_sl, :]), rPS(ob), rtt)
                    TTo(YM3[o_sl, j, :], PSB[ob][o_sl, :], tt[o_sl, :], ALU.mult, rPS(ob) + rtt, [("TT", j // 2)])
                    yield

            lg = lru_gen()
            agens = [att_gen(0), att_gen(1)]
            att_rounds = 4 * (nkc + LAG)
            stride = max(1, (att_rounds - 4) // 24) if LRU_SPREAD else 1
            rnd = 0
            lru_alive = True
            while agens or lru_alive:
                if lru_alive and (rnd % stride == 0 or not agens):
                    try:
                        next(lg)
                    except StopIteration:
                        lru_alive = False
                for g_ in list(agens):
                    try:
                        next(g_)
                    except StopIteration:
                        agens.remove(g_)
                rnd += 1

            dump(YM3[:, 0, :], [("TT", 0)], 5, ti)
            dump(YL3[:, 1, :], [("YL", 1)], 0, ti)
            dump(YL3[:, 2, :], [("YL", 2)], 2, ti)
            dump(YL3[:, 3, :], [("YL", 3)], 3, ti)
            dump(YM3[:, 1, :], [("TT", 0)], 4, ti)
            dump(YM3[:, 2, :], [("TT", 1)], 6, ti)
            dump(YM3[:, 3, :], [("TT", 1)], 7, ti)
            for c in range(8):
                if c % 4 == 0:
                    slot = use_block("wout")
                    Wv = W3[:, slot, :]
                bk = ga()
                if len(pending) > 1:
                    pending.pop(0)()
                for kc in range(8):
                    o = ((c % 4) * 8 + kc) * 128
                    if kc < 4:
                        MM(PSB[bk], Wv[:, o:o + 128], YL3[:, kc, :], kc == 0, kc == 7, [("W", slot), ("YL", kc)], rPS(bk))
                    else:
                        MM(PSB[bk], Wv[:, o:o + 128], YM3[:, kc - 4, :], kc == 0, kc == 7, [("W", slot), ("TT", (kc - 4) // 2)], rPS(bk))
                ACT(F3[:, c, :], PSB[bk], AF.Copy, rPS(bk), rF(c))
                sq = c % 2
                ACT(SQ[sq], PSB[bk], AF.Square, rPS(bk), rSQ(sq))
                pending.append(lambda c=c, sq=sq: MM(PSB[6], ONES[:], SQ[sq], c == 0, c == 7, rSQ(sq) + [("ONES",)], rPS(6)))
            flush_pending()
            norm_rstd(6, 1024, RS[:], rRS)
            for c in range(8):
                STT(F3[:, c, :], F3[:, c, :], GP[:, 8 + c:9 + c], RS[:], ALU.mult, ALU.mult, rF(c) + rRS + [("GP",)], rF(c))
                TTo(H3[:, c, :], H3[:, c, :], F3[:, c, :], ALU.add, [("H", c)] + rF(c), [("H", c)], eng="dve")

        rH_all = [("H", c) for c in range(8)]
        for ti in range(ntiles):
            t0 = ti * T
            for c in range(8):
                P.add("sp", lambda e, t0=t0, c=c: e.dma_start(out=H3[:, c, :], in_=xT3[:, c, t0:t0 + T]), reads=(), writes=[("H", c)], dma=("x", c))
            if ti == 0:
                ensure_loaded(1)
            if debug_stop == "ffn1":
                ffn(1, 0, 0, False, ti)
                gstate["next_use"] = (ti + 1) * len(tseq)
                gstate["next_load"] = max(gstate["next_load"], gstate["next_use"])
                store(H3[:, :, :], rH_all, ti)
                continue
            ffn(1, 0, 0, False, ti, skip_prenorm=(PREFETCH and ti > 0))
            mixer(ti)
            if debug_stop == "mixdump":
                gstate["next_use"] = (ti + 1) * len(tseq)
                gstate["next_load"] = max(gstate["next_load"], gstate["next_use"])
                continue
            if debug_stop == "mix":
                gstate["next_use"] = (ti + 1) * len(tseq)
                gstate["next_load"] = max(gstate["next_load"], gstate["next_use"])
                store(H3[:, :, :], rH_all, ti)
                continue
            ffn(2, 32, 16, True, ti, prefetch_next=(ti + 1 if (PREFETCH and ti + 1 < ntiles) else None))
            store(F3[:, :, :], rF_all, ti)
            if ti == 0:
                while pend_wb:
                    pend_wb.pop(0)()
                P.add("dve", lambda e: e.memset(VC[:, 3072:].rearrange("p (n x) -> p n x", x=192)[:, :, 64:128], 1.0),
                      reads=[("STG", 0), ("STG", 1)],
                      writes=[("VCones",), ("STG", 0), ("STG", 1)] + [("V", k) for k in range(4, 32)])
        P.add("sp", None, reads=[("OUT", ti) for ti in range(ntiles)] + dumps, writes=())

        P.finalize()
        with nc.Block() as block:
            @block.sync
            def _(e):
                P.emit("sp", e, eng_sems, dma_sems)

            @block.gpsimd
            def _(e):
                P.emit("pool", e, eng_sems, dma_sems)

            @block.tensor
            def _(e):
                P.emit("pe", e, eng_sems, dma_sems)

            @block.scalar
            def _(e):
                P.emit("act", e, eng_sems, dma_sems)

            @block.vector
            def _(e):
                P.emit("dve", e, eng_sems, dma_sems)
    return nc


def _lhsT_tiles(Wm):
    K, M = Wm.shape
    return Wm.reshape(K // 128, 128, M).transpose(1, 0, 2)


def pack_weights(inp):
    f32 = np.float32
    out = {}
    for f, pre in ((1, "w_ffn1"), (2, "w_ffn2")):
        Wg = np.asarray(inp[pre + "_gate"][0], f32)
        Wu = np.asarray(inp[pre + "_up"][0], f32)
        Wd = np.asarray(inp[pre + "_down"][0], f32)
        G = np.stack([Wg.reshape(8, 128, 22, 128), Wu.reshape(8, 128, 22, 128)], axis=0)
        G = G.reshape(2, 8, 128, 11, 2, 128)
        G = G.transpose(3, 2, 4, 0, 1, 5)
        out["gu%d" % f] = np.ascontiguousarray(G.reshape(11, 128, 4096))
        Wd4 = Wd.reshape(22, 128, 8, 128)
        for tag, (j0, nj) in zip("ab", HALVES):
            A = Wd4[j0:j0 + nj].reshape(nj, 128, 4, 2, 128)
            A = A.transpose(2, 1, 3, 0, 4)
            out["wd%d%s" % (f, tag)] = np.ascontiguousarray(A.reshape(4, 128, 2 * nj * 128))
    w_in = np.asarray(inp["w_in"][0], f32)
    cols = []
    for c in range(3):
        cols.append(w_in[:, 1024 + 128 * c:1024 + 128 * (c + 1)])
    for c in range(2):
        cols.append(w_in[:, 1408 + 128 * c:1408 + 128 * (c + 1)])
    kr = np.zeros((1024, 128), f32)
    kr[:, 64:96] = w_in[:, 1664:1696]
    kr[:, 96:112] = w_in[:, 1680:1696]
    kr[:, 112:128] = w_in[:, 1664:1680]
    cols.append(kr)
    cols.append(np.zeros((1024, 128), f32))
    cols.append(np.zeros((1024, 128), f32))
    for c in range(4):
        cols.append(w_in[:, 128 * c:128 * (c + 1)])
        cols.append(w_in[:, 512 + 128 * c:512 + 128 * (c + 1)])
    while len(cols) < 16:
        cols.append(np.zeros((1024, 128), f32))
    tiles = [_lhsT_tiles(c_).reshape(128, 1024) for c_ in cols]
    out["win"] = np.ascontiguousarray(np.stack([np.concatenate(tiles[4 * b:4 * b + 4], axis=1) for b in range(4)]))
    wq = np.asarray(inp["w_q_b"][0], f32)
    qt = []
    for h in range(8):
        m = np.zeros((384, 128), f32)
        m[:, 0:96] = wq[:, 96 * h:96 * h + 96]
        m[:, 96:112] = wq[:, 96 * h + 80:96 * h + 96]
        m[:, 112:128] = wq[:, 96 * h + 64:96 * h + 80]
        qt.append(_lhsT_tiles(m).reshape(128, 384))
    out["wqb"] = np.ascontiguousarray(np.concatenate(qt, axis=1)[None])
    wkv = np.asarray(inp["w_kv_b"][0], f32).reshape(256, 8, 128)
    kn = wkv[:, :, 0:64].reshape(256, 512)
    vv = wkv[:, :, 64:128].reshape(256, 512)
    kpart = _lhsT_tiles(kn).reshape(128, 1024)
    vpart = _lhsT_tiles(vv).reshape(128, 1024)
    out["wkvb"] = np.ascontiguousarray(np.concatenate([kpart, vpart], axis=1)[None])
    wo = np.asarray(inp["w_out"][0], f32)
    wo4 = wo.reshape(8, 128, 2, 4, 128)
    wo4 = wo4.transpose(2, 1, 3, 0, 4)
    out["wout"] = np.ascontiguousarray(wo4.reshape(2, 128, 4096))
    vec = np.zeros((128, NV), f32)

    def put(col, v):
        v = np.asarray(v, f32).reshape(-1, 128)
        vec[:, col:col + v.shape[0]] = v.T
    put(0, inp["g_ffn1_pre"][0]); put(8, inp["g_ffn1_post"][0]); put(16, inp["g_mix_pre"][0])
    put(24, inp["g_mix_post"][0]); put(32, inp["g_ffn2_pre"][0]); put(40, inp["g_ffn2_post"][0])
    put(48, inp["q_a_norm"][0]); put(51, inp["kv_a_norm"][0])
    cw = np.asarray(inp["conv_w"][0], f32)
    for k in range(4):
        put(53 + 4 * k, cw[k])
    put(69, inp["conv_b"][0]); put(73, inp["b_lru_a"][0]); put(77, inp["b_lru_x"][0]); put(81, inp["lru_lambda"][0])
    inv_freq = (1.0 / (np.float32(10000.0) ** (np.arange(0, 32, 2, dtype=np.float32) / np.float32(32)))).astype(f32)
    for r0 in (64, 80, 96, 112):
        vec[r0:r0 + 16, 85] = inv_freq
    vec[:, 86] = 1.0
    vec[96:112, 86] = -1.0
    out["vecs"] = vec
    wa = np.asarray(inp["w_lru_a"][0], f32)
    wx = np.asarray(inp["w_lru_x"][0], f32)
    bd = np.zeros((128, 2, 4, 128), f32)
    for g, wm in enumerate((wa, wx)):
        for c in range(4):
            for a in range(2):
                bd[64 * a:64 * a + 64, g, c, 64 * a:64 * a + 64] = wm[2 * c + a]
    out["bdw"] = np.ascontiguousarray(bd.reshape(128, 1024))
    return out


_NC_CACHE = {}


def kernel(**inputs):
    x = np.asarray(inputs["x"], np.float32)
    positions = np.asarray(inputs["positions"], np.int32)
    B = x.shape[0]
    packed = pack_weights(inputs)
    dbg = os.environ.get("MK_DEBUG_STOP") or None
    key = dbg
    if key not in _NC_CACHE:
        _NC_CACHE[key] = build(debug_stop=dbg)
    nc = _NC_CACHE[key]
    in_maps = []
    for b in range(B):
        m = {"xT": np.ascontiguousarray(x[b].T), "pos": np.ascontiguousarray(positions[b:b + 1])}
        m.update(packed)
        in_maps.append(m)
    res = run_bass_kernel_spmd(nc, in_maps, core_ids=list(range(B)))
    out = np.stack([np.asarray(res.results[b]["outT"]).T for b in range(B)])
    return np.ascontiguousarray(out.astype(np.float32))
```
